# Optimizing a Trainium2 kernel written in Bass

```python
import math
import jax, jax.numpy as jnp
from jax import lax
import numpy as np

D_MODEL = 1024
BATCH = 16
SEQ = 2048
DEPTH = 1

HEAD_DIM = 64
N_Q_HEADS = 8
N_KV_HEADS = 2
Q_PER_KV = N_Q_HEADS // N_KV_HEADS
ATTN_WIDTH = N_Q_HEADS * HEAD_DIM
KV_WIDTH = N_KV_HEADS * HEAD_DIM
N_FOURIER_GROUPS = 8
FOURIER_GROUP_DIM = 64
FOURIER_WIDTH = N_FOURIER_GROUPS * FOURIER_GROUP_DIM
MIX_WIDTH = ATTN_WIDTH + FOURIER_WIDTH
IN_PROJ_WIDTH = ATTN_WIDTH + 2 * KV_WIDTH + FOURIER_WIDTH
D_FF = 4 * D_MODEL
GRID_W = 64
AXIS_DIM = HEAD_DIM // 2
ROPE_THETA = 10000.0
Q_BLOCK = 128
NORM_EPS = 1e-6

kernel_name = "hymba_style_fnet_axial_gqa_block"


def rms_norm(x, g):
    xf = x.astype(jnp.float32)
    y = xf * lax.rsqrt(jnp.mean(xf * xf, axis=-1, keepdims=True) + NORM_EPS)
    return (y * g.astype(jnp.float32)).astype(x.dtype)


def axial_angles(seq_len):
    rows = seq_len // GRID_W
    row = jnp.repeat(jnp.arange(rows, dtype=jnp.int32), GRID_W)
    col = jnp.tile(jnp.arange(GRID_W, dtype=jnp.int32), rows)
    inv_freq = ROPE_THETA ** (-jnp.arange(0, AXIS_DIM, 2, dtype=jnp.float32) / AXIS_DIM)
    row_ang = row.astype(jnp.float32)[:, None] * inv_freq[None, :]
    col_ang = col.astype(jnp.float32)[:, None] * inv_freq[None, :]
    return row_ang, col_ang


def rotate_half_axis(x, ang):
    half = AXIS_DIM // 2
    c = jnp.cos(ang).astype(x.dtype)
    s = jnp.sin(ang).astype(x.dtype)
    x1, x2 = x[..., :half], x[..., half:]
    return jnp.concatenate([x1 * c - x2 * s, x1 * s + x2 * c], axis=-1)


def apply_axial_rope(x, row_ang, col_ang):
    return jnp.concatenate([rotate_half_axis(x[..., :AXIS_DIM], row_ang),
                            rotate_half_axis(x[..., AXIS_DIM:], col_ang)], axis=-1)


def gqa_axial_attention(q, k, v, q_norm_g, k_norm_g):
    B, S, _ = q.shape
    q = q.reshape(B, S, N_Q_HEADS, HEAD_DIM)
    k = k.reshape(B, S, N_KV_HEADS, HEAD_DIM)
    v = v.reshape(B, S, N_KV_HEADS, HEAD_DIM)
    q = rms_norm(q, q_norm_g).transpose(0, 2, 1, 3)
    k = rms_norm(k, k_norm_g).transpose(0, 2, 1, 3)
    v = v.transpose(0, 2, 1, 3)
    row_ang, col_ang = axial_angles(S)
    q = apply_axial_rope(q, row_ang, col_ang)
    k = apply_axial_rope(k, row_ang, col_ang)
    q = q * jnp.asarray(HEAD_DIM ** -0.5, dtype=q.dtype)
    n_blk = S // Q_BLOCK
    qb = q.reshape(B, N_KV_HEADS, Q_PER_KV, n_blk, Q_BLOCK, HEAD_DIM)
    qb = jnp.moveaxis(qb, 3, 0)

    def one_block(q_blk):
        s = jnp.einsum('bkgqd,bksd->bkgqs', q_blk, k).astype(jnp.float32)
        p = jax.nn.softmax(s, axis=-1).astype(v.dtype)
        return jnp.einsum('bkgqs,bksd->bkgqd', p, v)

    o = lax.map(one_block, qb)
    o = o.transpose(1, 0, 4, 2, 3, 5)
    return o.reshape(B, S, ATTN_WIDTH)


def fourier_mixer(u, w_fourier):
    B, S, _ = u.shape
    ug = u.reshape(B, S, N_FOURIER_GROUPS, FOURIER_GROUP_DIM).astype(jnp.float32)
    f = jnp.fft.fftn(ug, axes=(1, 3), norm='ortho').real.astype(u.dtype)
    y = jnp.einsum('bsgc,gcd->bsgd', f, w_fourier)
    return y.reshape(B, S, FOURIER_WIDTH)


def setup_inputs(seed: int = 0) -> dict:
    key = jax.random.key(seed)
    ks = jax.random.split(key, 12)
    f32 = jnp.float32
    x = jax.random.normal(ks[0], (BATCH, SEQ, D_MODEL), f32)
    mix_norm_g = 1.0 + 0.05 * jax.random.normal(ks[1], (D_MODEL,), f32)
    w_in = jax.random.normal(ks[2], (D_MODEL, IN_PROJ_WIDTH), f32) * D_MODEL ** -0.5
    q_norm_g = 1.0 + 0.05 * jax.random.normal(ks[3], (HEAD_DIM,), f32)
    k_norm_g = 1.0 + 0.05 * jax.random.normal(ks[4], (HEAD_DIM,), f32)
    w_fourier = jax.random.normal(ks[5], (N_FOURIER_GROUPS, FOURIER_GROUP_DIM, FOURIER_GROUP_DIM), f32) * FOURIER_GROUP_DIM ** -0.5
    w_out = jax.random.normal(ks[6], (MIX_WIDTH, D_MODEL), f32) * MIX_WIDTH ** -0.5
    mlp_norm_g = 1.0 + 0.05 * jax.random.normal(ks[7], (D_MODEL,), f32)
    w_up = jax.random.normal(ks[8], (D_MODEL, D_FF), f32) * D_MODEL ** -0.5
    w_down = jax.random.normal(ks[9], (D_FF, D_MODEL), f32) * D_FF ** -0.5
    final_norm_g = 1.0 + 0.05 * jax.random.normal(ks[10], (D_MODEL,), f32)
    return {"x": x, "mix_norm_g": mix_norm_g, "w_in": w_in, "q_norm_g": q_norm_g,
            "k_norm_g": k_norm_g, "w_fourier": w_fourier, "w_out": w_out,
            "mlp_norm_g": mlp_norm_g, "w_up": w_up, "w_down": w_down,
            "final_norm_g": final_norm_g}


def reference(x, mix_norm_g, w_in, q_norm_g, k_norm_g, w_fourier, w_out,
              mlp_norm_g, w_up, w_down, final_norm_g):
    for _ in range(DEPTH):
        h = rms_norm(x, mix_norm_g)
        proj = jnp.einsum('bsd,de->bse', h, w_in)
        q = proj[..., :ATTN_WIDTH]
        k = proj[..., ATTN_WIDTH:ATTN_WIDTH + KV_WIDTH]
        v = proj[..., ATTN_WIDTH + KV_WIDTH:ATTN_WIDTH + 2 * KV_WIDTH]
        u = proj[..., ATTN_WIDTH + 2 * KV_WIDTH:]
        attn_out = gqa_axial_attention(q, k, v, q_norm_g, k_norm_g)
        four_out = fourier_mixer(u, w_fourier)
        mixed = jnp.concatenate([attn_out, four_out], axis=-1)
        x = x + jnp.einsum('bse,ed->bsd', mixed, w_out)
        h = rms_norm(x, mlp_norm_g)
        z = jnp.einsum('bsd,df->bsf', h, w_up)
        z = jnp.square(jax.nn.relu(z))
        x = x + jnp.einsum('bsf,fd->bsd', z, w_down)
    return rms_norm(x, final_norm_g)
```

```python
from contextlib import ExitStack
import numpy as np
import ml_dtypes
import concourse.bass as bass
import concourse.mybir as mybir
from concourse.bass_utils import run_bass_kernel_spmd

F32 = mybir.dt.float32
BF16 = mybir.dt.bfloat16
ALU = mybir.AluOpType
AF = mybir.ActivationFunctionType
AX = mybir.AxisListType

PE, ACT, DVE, POOL, SP = "tensor", "scalar", "vector", "gpsimd", "sync"
ENGINES = (PE, ACT, DVE, POOL, SP)

N_CORES = 8
SEQ = 2048
DM = 1024
DFF = 4096
NSEQ = 2
EPS = 1e-6


class Buf:
    __slots__ = ("name", "writer", "readers")

    def __init__(self, name):
        self.name = name
        self.writer = None
        self.readers = []


class Op:
    __slots__ = ("eng", "fn", "deps", "is_dma", "key", "signal", "tok")

    def __init__(self, eng, fn, is_dma=False, key=None):
        self.eng = eng
        self.fn = fn
        self.deps = {}
        self.is_dma = is_dma
        self.key = key
        self.signal = False
        self.tok = None


class Prog:
    def __init__(self):
        self.ops = {e: [] for e in ENGINES}
        self.final_waits = []

    def _track(self, op, reads, writes):
        deps = []
        for b in reads:
            if b.writer is not None:
                deps.append((b.writer, "RAW"))
        for b in writes:
            if b.writer is not None:
                deps.append((b.writer, "WAW"))
            for r in b.readers:
                deps.append((r, "WAR"))
        for d, kind in deps:
            if d is op:
                continue
            if d.eng == op.eng and not d.is_dma and not op.is_dma and op.eng != POOL:
                if kind == "WAW" or op.eng == PE:
                    continue
            if id(d) not in op.deps:
                op.deps[id(d)] = d
                d.signal = True
        for b in reads:
            b.readers.append(op)
        for b in writes:
            b.writer = op
            b.readers = []

    def op(self, eng, fn, reads=(), writes=()):
        o = Op(eng, fn)
        self._track(o, reads, writes)
        self.ops[eng].append(o)
        return o

    def dma(self, eng, fn, key, reads=(), writes=(), final=False):
        o = Op(eng, fn, is_dma=True, key=key)
        o.signal = True
        self._track(o, reads, writes)
        self.ops[eng].append(o)
        if final:
            self.final_waits.append(o)
        return o

    def emit(self, nc, stack):
        keycount = {}
        for e in ENGINES:
            for i, o in enumerate(self.ops[e]):
                if o.is_dma:
                    keycount[o.key] = keycount.get(o.key, 0) + 16
                    o.tok = ("d_" + o.key, keycount[o.key])
                else:
                    o.tok = ("e_" + e, i + 1)

        def plan(e):
            waited = {}
            out = []
            for o in self.ops[e]:
                need = {}
                for d in o.deps.values():
                    sn, v = d.tok
                    if need.get(sn, 0) < v:
                        need[sn] = v
                ws = []
                for sn, v in need.items():
                    if waited.get(sn, 0) >= v:
                        continue
                    ws.append((sn, v))
                    waited[sn] = v
                out.append(ws)
            fin = {}
            if e == SP:
                for d in self.final_waits:
                    sn, v = d.tok
                    if fin.get(sn, 0) < v:
                        fin[sn] = v
            return out, list(fin.items())

        plans = {e: plan(e) for e in ENGINES}
        used = {}
        for e in ENGINES:
            ws_list, fin = plans[e]
            for ws in ws_list:
                for sn, v in ws:
                    used.setdefault(sn, set()).add(v)
            for sn, v in fin:
                used.setdefault(sn, set()).add(v)
        final_val = {}
        for e in ENGINES:
            sn = "e_" + e
            vals = sorted(used.get(sn, ()))
            final_val[sn] = {v: r + 1 for r, v in enumerate(vals)}
        sems = {}
        for sn in used:
            sems[sn] = stack.enter_context(nc.semaphore("s_" + sn))
        for e in ENGINES:
            for o in self.ops[e]:
                if o.is_dma and o.tok[0] not in sems:
                    sems[o.tok[0]] = stack.enter_context(nc.semaphore("s_" + o.tok[0]))
        block = stack.enter_context(nc.Block())
        prog = self

        def xl(sn, v):
            return final_val[sn][v] if sn in final_val and sn.startswith("e_") else v

        def make(e):
            def body(eng):
                ws_list, fin = plans[e]
                for o, ws in zip(prog.ops[e], ws_list):
                    for sn, v in ws:
                        eng.wait_ge(sems[sn], xl(sn, v))
                    ins = o.fn(eng)
                    sn, v = o.tok
                    if o.is_dma:
                        ins.then_inc(sems[sn], 16)
                    elif v in final_val.get(sn, ()):
                        ins.then_inc(sems[sn], 1)
                for sn, v in fin:
                    eng.wait_ge(sems[sn], xl(sn, v))
            return body

        block.tensor(make(PE))
        block.scalar(make(ACT))
        block.vector(make(DVE))
        block.gpsimd(make(POOL))
        block.sync(make(SP))
        self.nsems = len(sems)
        self.nsignals = {sn: len(m) for sn, m in final_val.items()}


class _Stop(Exception):
    pass


_DBG = {"nolate": False}


def build_nc(nseq=NSEQ, stop=None, dumps=()):
    nc = bass.Bass("TRN2", target_bir_lowering=False)
    NTOK = nseq * SEQ

    def din(name, shape, dt=F32):
        return nc.dram_tensor(name, shape, dt, kind="ExternalInput").ap()

    x_d = din("x", [NTOK, DM])
    w_in_d = din("w_in", [DM, 1280])
    w_out_d = din("w_out", [DM, DM])
    w_up_d = din("w_up", [DM, DFF])
    w_down_d = din("w_down", [DFF, DM])
    wf_d = din("wf", [512, 64])
    gmix_d = din("g_mix", [1, DM])
    gmlp_d = din("g_mlp", [1, DM])
    gfin_d = din("g_fin", [1, DM])
    gqk_d = din("g_qk", [1, 256])
    dftc_d = din("dftc", [SEQ, SEQ], BF16)
    dfts_d = din("dfts", [SEQ, SEQ], BF16)
    ropec_d = din("ropec", [SEQ, 64])
    ropes_d = din("ropes", [SEQ, 64])
    ccbd_d = din("ccbd", [128, 128], BF16)
    scbd_d = din("scbd", [128, 128], BF16)
    out_d = nc.dram_tensor("out", [NTOK, DM], F32, kind="ExternalOutput").ap()
    win_bf_d = nc.dram_tensor("win_bf", [DM, 1280], BF16, kind="Internal").ap()
    wout_bf_d = nc.dram_tensor("wout_bf", [DM, DM], BF16, kind="Internal").ap()
    wup_bf_d = nc.dram_tensor("wup_bf", [DM, DFF], BF16, kind="Internal").ap()
    wdown_bf_d = nc.dram_tensor("wdown_bf", [DFF, DM], BF16, kind="Internal").ap()

    P = Prog()
    st = ExitStack()
    with st:
        def sb(name, shape, dt):
            return st.enter_context(nc.sbuf_tensor(name, shape, dt))

        ident = sb("ident", [128, 128], BF16)
        g1 = sb("g1", [128, 8], F32)
        g2 = sb("g2", [128, 8], F32)
        gf_bc = sb("gf_bc", [128, DM], F32)
        gqk = sb("gqk", [128, 4, 64], F32)
        gmax = sb("gmax", [128, 4], F32)
        nbias = sb("nbias", [128, 1], F32)
        epst = sb("epst", [128, 1], F32)
        ccbd = sb("ccbd_sb", [128, 128], BF16)
        scbd = sb("scbd_sb", [128, 128], BF16)
        bdm = sb("bdm", [128, 8, 128], BF16)
        wout = sb("wout_sb", [128, 8, DM], BF16)
        vp = sb("vp", [128, 16, 256], BF16)
        qT = sb("qT", [128, 4, SEQ], BF16)
        kT = sb("kT", [128, SEQ], BF16)
        ocp = sb("ocp", [128, 1024], F32)
        U = sb("U", [128, 16, 512], BF16)
        r1 = sb("r1", [128, 32 * 512], BF16)
        win = r1[:, 0:8 * 1280].rearrange("p (k n) -> p k n", k=8)
        zT = r1[:, :].rearrange("p (f t) -> p f t", t=512)
        hT_b = r1[:, 10240:14336].rearrange("p (k t) -> p k t", t=512)
        xbuf = sb("xbuf", [128, 4, DM], F32)
        xn = sb("xn", [128, 4, DM], BF16)
        hT = sb("hT", [128, 8, 512], BF16)
        junk = sb("junk", [128, DM], BF16)
        fz = sb("fz", [128, 8], BF16)
        ss = sb("ss", [128, 4], F32)
        lnv = sb("lnv", [128, 4], F32)
        rstd = sb("rstd", [128, 4], F32)
        r2 = sb("r2", [128, 12288], BF16)
        r2f = r2[:, :].bitcast(F32)
        rC = r2f[:, 0:256].rearrange("p (t d) -> p t d", d=64)
        rS = r2f[:, 256:512].rearrange("p (t d) -> p t d", d=64)
        Cq = r2f[:, 512:768].rearrange("p (t d) -> p t d", d=64)
        Sq = r2f[:, 768:1024].rearrange("p (t d) -> p t d", d=64)
        Ck = r2f[:, 1024:1280].rearrange("p (t d) -> p t d", d=64)
        Sk = r2f[:, 1280:1536].rearrange("p (t d) -> p t d", d=64)
        SETS = []
        for si in range(2):
            base = 1536 + si * 2240
            SETS.append(dict(
                nq=r2f[:, base:base + 640],
                t1=r2f[:, base + 640:base + 1280],
                t2=r2f[:, base + 1280:base + 1920],
                qk=r2[:, 2 * (base + 1920):2 * (base + 1920) + 640],
                ssq=r2f[:, 6016 + si * 48:6016 + si * 48 + 10],
                lnq=r2f[:, 6016 + si * 48 + 16:6016 + si * 48 + 26],
                rs10=r2f[:, 6016 + si * 48 + 32:6016 + si * 48 + 42],
            ))
        NPT = 4
        PT = [r2[:, i * 1024:(i + 1) * 1024] for i in range(NPT)]
        NDS = 4
        dsl = [r2[:, 4096 + i * 2048: 4096 + (i + 1) * 2048].rearrange("p (a j) -> p a j", j=512)
               for i in range(NDS)]
        mixedT = sb("mixedT", [128, 8, 512], BF16)
        atbt = sb("atbt", [128, 8, 512], BF16)
        wfz = atbt[:, 0:2, :].rearrange("p a t -> p (a t)").bitcast(F32).rearrange("p (i d) -> p i d", d=128)
        wfb = atbt[:, 2, :].rearrange("p (i d) -> p i d", d=128)
        gtmp = atbt[:, 3, :].bitcast(F32).rearrange("p (i d) -> p i d", d=64)
        identz = atbt[:, 4, 0:128]
        NWS = 3
        wsl = [sb("wsl%d" % i, [128, 4096], BF16) for i in range(NWS)]
        rcp = sb("rcp", [128, 512], F32)
        z1 = [sb("z1_%d" % i, [128, 512], BF16) for i in range(2)]

        ps = st.enter_context(nc.psum_tensor("ps", [128, 4096], F32))
        psb = ps[:, :].bitcast(BF16)
        HB = [Buf("bank%d" % i) for i in range(8)]

        def hb(i, n=1):
            return HB[i:i + n]

        B = {}

        def T(name):
            if name not in B:
                B[name] = Buf(name)
            return B[name]

        G1 = T("guard_r1")
        G2 = T("guard_r2")

        P.op(POOL, lambda e: e.memset(identz, 0.0), writes=[T("atbt4")])
        P.op(POOL, lambda e: e.affine_select(out=ident[:], in_=identz, compare_op=ALU.not_equal, fill=1.0,
                                             base=0, pattern=[[-1, 128]], channel_multiplier=1),
             reads=[T("atbt4")], writes=[T("ident")])
        P.op(POOL, lambda e: e.memset(vp[:, :, 64:192], 1.0), writes=[T("vp_ones")])
        P.op(POOL, lambda e: e.memset(epst[:], EPS), writes=[T("epst")])
        P.op(POOL, lambda e: e.memset(ocp[:], 0.0), writes=[T("ocp")])
        P.op(POOL, lambda e: e.memset(wfz, 0.0), writes=[T("atbt0"), T("atbt1")])

        def cast_piece(dst, src, r0, r1_, key, tokname):
            P.dma(POOL, lambda e: e.dma_start(out=dst[r0:r1_, :], in_=src[r0:r1_, :]), key, writes=[T(tokname)])
        for i in range(2):
            cast_piece(win_bf_d, w_in_d, i * 512, (i + 1) * 512, "c_win%d" % i, "win_bf%d" % i)
        for i in range(2):
            cast_piece(wout_bf_d, w_out_d, i * 512, (i + 1) * 512, "c_wout%d" % i, "wout_bf%d" % i)
        late_casts = []
        wup_bf_v = wup_bf_d.rearrange("r (a n) -> (r a) n", n=1024)
        w_up_v = w_up_d.rearrange("r (a n) -> (r a) n", n=1024)
        for i in range(8):
            late_casts.append((wup_bf_v, w_up_v, i * 512, (i + 1) * 512, "c_wup%d" % i, "wup_bf%d" % i))
        for i in range(8):
            late_casts.append((wdown_bf_d, w_down_d, i * 512, (i + 1) * 512, "c_wdn%d" % i, "wdown_bf%d" % i))
        if _DBG["nolate"]:
            late_casts = []

        P.dma(SP, lambda e: e.dma_start(out=g1[:], in_=gmix_d.rearrange("o (k p) -> p (o k)", p=128),
                                        allow_slow_non_contiguous=True), "k_g1", writes=[T("g1")])
        P.dma(SP, lambda e: e.dma_start(out=gqk[:].rearrange("p a d -> p (a d)"), in_=gqk_d.partition_broadcast(128)),
              "k_gqk", writes=[T("gqk")])
        P.dma(SP, lambda e: e.dma_start(out=ccbd[:], in_=ccbd_d), "k_ccbd", writes=[T("ccbd")])
        P.dma(SP, lambda e: e.dma_start(out=scbd[:], in_=scbd_d), "k_scbd", writes=[T("scbd")])
        for i in range(4):
            P.dma(SP, lambda e, i=i: e.dma_start(out=wfz[0:64, i, 0:64], in_=wf_d[(2 * i) * 64:(2 * i + 1) * 64, :]),
                  "k_wf%d" % (2 * i), reads=[T("atbt0"), T("atbt1")], writes=[T("wfz_a%d" % i)])
            P.dma(SP, lambda e, i=i: e.dma_start(out=wfz[64:128, i, 64:128], in_=wf_d[(2 * i + 1) * 64:(2 * i + 2) * 64, :]),
                  "k_wf%d" % (2 * i + 1), reads=[T("atbt0"), T("atbt1")], writes=[T("wfz_b%d" % i)])
        P.dma(SP, lambda e: e.dma_start(out=g2[:], in_=gmlp_d.rearrange("o (k p) -> p (o k)", p=128),
                                        allow_slow_non_contiguous=True), "k_g2", writes=[T("g2")])
        P.dma(SP, lambda e: e.dma_start(out=gf_bc[:], in_=gfin_d.partition_broadcast(128)), "k_gf", writes=[T("gf")])

        P.op(DVE, lambda e: e.tensor_scalar(gtmp, gqk[:], -1.0, None, op0=ALU.mult), reads=[T("gqk")], writes=[T("atbt3")])
        P.op(DVE, lambda e: e.tensor_tensor(out=gtmp, in0=gtmp, in1=gqk[:], op=ALU.max), reads=[T("atbt3"), T("gqk")], writes=[T("atbt3")])
        P.op(DVE, lambda e: e.tensor_reduce(out=gmax[:], in_=gtmp, axis=AX.X, op=ALU.max), reads=[T("atbt3")], writes=[T("gmax")])
        P.op(DVE, lambda e: e.tensor_tensor(out=nbias[:], in0=gmax[:, 0:1], in1=gmax[:, 2:3], op=ALU.mult), reads=[T("gmax")], writes=[T("nbias")])
        P.op(DVE, lambda e: e.tensor_scalar(nbias[:], nbias[:], -8.0, None, op0=ALU.mult), reads=[T("nbias")], writes=[T("nbias")])

        gqs = sb("gqs", [128, 4, 64], F32)

        def mk_gqs(e):
            e.tensor_scalar(gqs[:, 0:2, :], gqk[:, 0:2, :], 0.125, None, op0=ALU.mult)
            return e.tensor_copy(gqs[:, 2:4, :], gqk[:, 2:4, :])
        P.op(DVE, mk_gqs, reads=[T("gqk")], writes=[T("gqs")])

        P.op(DVE, lambda e: e.tensor_copy(wfb, wfz),
             reads=[T("atbt0"), T("atbt1")] + [T("wfz_a%d" % i) for i in range(4)] + [T("wfz_b%d" % i) for i in range(4)], writes=[T("atbt2")])

        def bd_mm(e):
            for i in range(4):
                e.matmul(ps[:, i * 128:(i + 1) * 128], lhsT=ccbd[:, :], rhs=wfb[:, i, :], start=True, stop=True)
            for i in range(4):
                r = e.matmul(ps[:, 512 + i * 128:512 + (i + 1) * 128], lhsT=scbd[:, :], rhs=wfb[:, i, :], start=True, stop=True)
            return r
        P.op(PE, bd_mm, reads=[T("ccbd"), T("scbd"), T("atbt2")], writes=hb(0, 2))
        P.op(DVE, lambda e: e.tensor_copy(bdm[:].rearrange("p a d -> p (a d)"), ps[:, 0:1024]), reads=hb(0, 2), writes=[T("bdm")])

        P.dma(SP, lambda e: e.dma_start(out=wout[:], in_=wout_bf_d.rearrange("(k p) n -> p k n", p=128)), "k_wout",
              reads=[T("wout_bf0"), T("wout_bf1")], writes=[T("wout")])

        def fence():
            P.op(POOL, lambda e: e.memset(fz[:], 0.0), writes=[G1, G2, T("fz")])

        def rms_stats(src_ap, col, src_toks, tag):
            P.op(ACT, lambda e: e.activation(out=junk[:], in_=src_ap, func=AF.Square, accum_out=ss[:, col:col + 1]),
                 reads=src_toks, writes=[T("junk"), T("ss%d" % col)])
            P.op(ACT, lambda e: e.activation(out=lnv[:, col:col + 1], in_=ss[:, col:col + 1], func=AF.Ln,
                                             bias=epst[:], scale=1.0 / DM),
                 reads=[T("ss%d" % col), T("epst")], writes=[T("lnv%d" % col)])
            P.op(ACT, lambda e: e.activation(out=rstd[:, col:col + 1], in_=lnv[:, col:col + 1], func=AF.Exp, scale=-0.5),
                 reads=[T("lnv%d" % col)], writes=[T("rstd%d" % col)])

        def transpose_kc(kc, gvec, gtok, hdst, htag, extra=()):
            h = kc % 2

            def tr(e):
                for t in range(4):
                    r = e.transpose(psb[:, h * 1024 + t * 128: h * 1024 + (t + 1) * 128],
                                    xn[:, t, kc * 128:(kc + 1) * 128], ident[:])
                return r
            P.op(PE, tr, reads=[T("xn%d" % t) for t in range(4)] + [T("ident")], writes=[HB[h]])
            P.op(DVE, lambda e: e.tensor_scalar(hdst[:, kc, :], psb[:, h * 1024:h * 1024 + 512],
                                                gvec[:, kc:kc + 1], None, op0=ALU.mult),
                 reads=[HB[h], gtok] + list(extra), writes=[T("%s%d" % (htag, kc))])

        def transposes_to_hT(gvec, gtok):
            for kc in range(8):
                transpose_kc(kc, gvec, gtok, hT, "hT")

        def phase_a(b):
            P.dma(SP, lambda e: e.dma_start(out=win, in_=win_bf_d.rearrange("(k p) n -> p k n", p=128)), "k_win",
                  reads=[T("win_bf0"), T("win_bf1"), G1], writes=[T("win")])
            hbufs = [(hT, "hT", ()), (hT_b, "hTb", (G1,))]

            def head_load(g):
                r0 = b * SEQ + g * 512
                P.dma(SP, lambda e: e.dma_start(out=xbuf[:], in_=x_d[r0:r0 + 512, :].rearrange("(t p) d -> p t d", p=128)),
                      "xl", writes=[T("xb%d" % t) for t in range(4)])

            def head_tile(t):
                rms_stats(xbuf[:, t, :], t, [T("xb%d" % t)], "a")
                P.op(DVE, lambda e: e.tensor_scalar(xn[:, t, :], xbuf[:, t, :], rstd[:, t:t + 1], None, op0=ALU.mult),
                     reads=[T("xb%d" % t), T("rstd%d" % t)], writes=[T("xn%d" % t)])

            def tables(g):
                s0 = g * 512
                P.dma(SP, lambda e: e.dma_start(out=rC, in_=ropec_d[s0:s0 + 512, :].rearrange("(t p) d -> p t d", p=128)),
                      "rc", reads=[G2], writes=[T("rC")])
                P.dma(SP, lambda e: e.dma_start(out=rS, in_=ropes_d[s0:s0 + 512, :].rearrange("(t p) d -> p t d", p=128)),
                      "rs", reads=[G2], writes=[T("rS")])

                def bc(i):
                    return gqs[:, i, :].unsqueeze(1).to_broadcast([128, 4, 64])
                P.op(POOL, lambda e: e.tensor_tensor(out=Cq, in0=rC, in1=bc(0), op=ALU.mult),
                     reads=[T("rC"), T("gqs"), G2], writes=[T("Cq")])
                P.op(POOL, lambda e: e.tensor_tensor(out=Sq, in0=rS, in1=bc(1), op=ALU.mult),
                     reads=[T("rS"), T("gqs"), G2], writes=[T("Sq")])
                P.op(POOL, lambda e: e.tensor_tensor(out=Ck, in0=rC, in1=bc(2), op=ALU.mult),
                     reads=[T("rC"), T("gqs"), G2], writes=[T("Ck")])
                P.op(POOL, lambda e: e.tensor_tensor(out=Sk, in0=rS, in1=bc(3), op=ALU.mult),
                     reads=[T("rS"), T("gqs"), G2], writes=[T("Sk")])

            def inproj(g, t):
                hsrc, htag, hextra = hbufs[g % 2]
                pb = 2 + 3 * (t % 2)

                def f(e):
                    for kc in range(8):
                        for (c0, c1) in ((0, 512), (512, 1024), (1024, 1280)):
                            r = e.matmul(ps[:, pb * 512 + c0: pb * 512 + c1], lhsT=hsrc[:, kc, t * 128:(t + 1) * 128],
                                         rhs=win[:, kc, c0:c1], start=(kc == 0), stop=(kc == 7))
                    return r
                P.op(PE, f, reads=[T("%s%d" % (htag, kc)) for kc in range(8)] + [T("win"), G1], writes=hb(pb, 3))

            def post(g, t):
                TT = g * 4 + t
                pb = 2 + 3 * (t % 2)
                pin = ps[:, pb * 512: pb * 512 + 1280]
                S_ = SETS[TT % 2]
                sx = "%d" % (TT % 2)
                nq, t1, t2, qk_tm, ssq, lnq, rs10 = S_["nq"], S_["t1"], S_["t2"], S_["qk"], S_["ssq"], S_["lnq"], S_["rs10"]
                sq = t1
                P.op(ACT, lambda e: e.activation(out=sq, in_=pin[:, 0:640], func=AF.Square),
                     reads=hb(pb, 2) + [G2], writes=[T("t1" + sx)])
                P.op(DVE, lambda e: e.tensor_reduce(out=ssq, in_=sq.rearrange("p (h d) -> p h d", d=64), axis=AX.X, op=ALU.add),
                     reads=[T("t1" + sx), G2], writes=[T("ssq" + sx)])
                P.op(ACT, lambda e: e.activation(out=lnq, in_=ssq, func=AF.Ln, bias=epst[:], scale=1.0 / 64),
                     reads=[T("ssq" + sx), T("epst"), G2], writes=[T("lnq" + sx)])
                P.op(ACT, lambda e: e.activation(out=rs10, in_=lnq, func=AF.Exp, scale=-0.5),
                     reads=[T("lnq" + sx), G2], writes=[T("rs10" + sx)])
                P.op(DVE, lambda e: e.tensor_tensor(out=nq.rearrange("p (h d) -> p h d", d=64),
                                                    in0=pin[:, 0:640].rearrange("p (h d) -> p h d", d=64),
                                                    in1=rs10.unsqueeze(2).to_broadcast([128, 10, 64]), op=ALU.mult),
                     reads=hb(pb, 2) + [T("rs10" + sx), G2], writes=[T("nq" + sx)])

                P.op(ACT, lambda e: e.activation(
                    out=vp[:, TT, :].rearrange("p (a d) -> p a d", d=64)[:, 0:4:3, :],
                    in_=pin[:, 640:768].rearrange("p (a d) -> p a d", d=64), func=AF.Copy),
                    reads=hb(pb, 2) + [T("vp_ones")], writes=[T("vp")])
                P.op(ACT, lambda e: e.activation(out=U[:, TT, :], in_=pin[:, 768:1280], func=AF.Copy),
                     reads=hb(pb + 1, 2), writes=[T("U")])

                def mul_c(e):
                    e.tensor_tensor(out=t1[:, 0:512].rearrange("p (h d) -> p h d", d=64),
                                    in0=nq[:, 0:512].rearrange("p (h d) -> p h d", d=64),
                                    in1=Cq[:, t, :].unsqueeze(1).to_broadcast([128, 8, 64]), op=ALU.mult)
                    return e.tensor_tensor(out=t1[:, 512:640].rearrange("p (h d) -> p h d", d=64),
                                           in0=nq[:, 512:640].rearrange("p (h d) -> p h d", d=64),
                                           in1=Ck[:, t, :].unsqueeze(1).to_broadcast([128, 2, 64]), op=ALU.mult)
                P.op(DVE, mul_c, reads=[T("nq" + sx), T("Cq"), T("Ck"), T("ssq" + sx), G2], writes=[T("t1" + sx)])

                def mul_s(e):
                    r = None
                    for (c0, c1, nh, tab) in ((0, 512, 8, Sq), (512, 640, 2, Sk)):
                        av = nq[:, c0:c1].rearrange("p (h x f d) -> p h x f d", x=2, f=2, d=16)
                        ov = t2[:, c0:c1].rearrange("p (h x f d) -> p h x f d", x=2, f=2, d=16)
                        tv = tab[:, t, :].rearrange("p (x f d) -> p x f d", x=2, f=2)
                        e.tensor_tensor(out=ov[:, :, :, 0, :], in0=av[:, :, :, 1, :],
                                        in1=tv[:, :, 0, :].unsqueeze(1).to_broadcast([128, nh, 2, 16]), op=ALU.mult)
                        r = e.tensor_tensor(out=ov[:, :, :, 1, :], in0=av[:, :, :, 0, :],
                                            in1=tv[:, :, 1, :].unsqueeze(1).to_broadcast([128, nh, 2, 16]), op=ALU.mult)
                    return r
                P.op(POOL, mul_s, reads=[T("nq" + sx), T("Sq"), T("Sk"), G2], writes=[T("t2" + sx)])

            def post_add(g, t):
                TT = g * 4 + t
                S_ = SETS[TT % 2]
                sx = "%d" % (TT % 2)
                t1, t2, qk_tm = S_["t1"], S_["t2"], S_["qk"]
                P.op(DVE, lambda e: e.tensor_tensor(out=qk_tm, in0=t1, in1=t2, op=ALU.add),
                     reads=[T("t1" + sx), T("t2" + sx), G2], writes=[T("qk" + sx)])


            def post_b(g, t):
                TT = g * 4 + t
                sx = "%d" % (TT % 2)
                qk_tm = SETS[TT % 2]["qk"]

                def trqk(e):
                    for i in range(5):
                        r = e.transpose(psb[:, 1024 + i * 128: 1024 + (i + 1) * 128], qk_tm[:, i * 128:(i + 1) * 128], ident[:])
                    return r
                P.op(PE, trqk, reads=[T("qk" + sx), T("ident"), G2], writes=hb(1, 1))
                P.op(ACT, lambda e: e.activation(out=qT[:, :, TT * 128:(TT + 1) * 128],
                                                 in_=psb[:, 1024:1536].rearrange("p (j s) -> p j s", s=128), func=AF.Copy),
                     reads=hb(1, 1), writes=[T("qT")])

                def kcopy(e):
                    e.activation(out=kT[0:64, TT * 128:(TT + 1) * 128], in_=psb[0:64, 1536:1664], func=AF.Copy)
                    return e.activation(out=kT[64:128, TT * 128:(TT + 1) * 128], in_=psb[64:128, 1536:1664], func=AF.Copy)
                P.op(ACT, kcopy, reads=hb(1, 1), writes=[T("kT")])

            head_load(0)
            for t in range(4):
                head_tile(t)
            for kc in range(8):
                transpose_kc(kc, g1, T("g1"), *hbufs[0][:2], extra=hbufs[0][2])
            tiles = [(g, t) for g in range(4) for t in range(4)]
            for i, (g, t) in enumerate(tiles):
                if t == 0:
                    tables(g)
                    if g + 1 < 4:
                        head_load(g + 1)
                inproj(g, t)
                if i >= 2:
                    post_b(*tiles[i - 2])
                if b == 0 and late_casts:
                    cast_piece(*late_casts.pop(0))
                if g + 1 < 4:
                    if t < 2:
                        head_tile(2 * t)
                        head_tile(2 * t + 1)
                    else:
                        hd, ht, hx = hbufs[(g + 1) % 2]
                        for kc in range(4 * (t - 2), 4 * (t - 2) + 4):
                            transpose_kc(kc, g1, T("g1"), hd, ht, extra=hx)
                post(g, t)
                if i >= 1:
                    post_add(*tiles[i - 1])
            post_add(*tiles[15])
            post_b(*tiles[14])
            post_b(*tiles[15])

        ctr = {"pt": 0, "ds": 0, "ws": 0, "z1": 0}

        def chunk(b, c, stage=lambda n: None, prev_tail=(), pre_done=0):
            prev_tail = list(prev_tail)
            j0 = c * 512
            r0 = b * SEQ + c * 512
            steps = [(j, kb) for j in range(4) for kb in range(16)]

            def qk_op(s):
                j, kb = steps[s]
                sg = s % 2

                def f(e):
                    e.matmul(ps[:, (2 * sg) * 512:(2 * sg + 1) * 512], lhsT=kT[0:64, kb * 128:(kb + 1) * 128],
                             rhs=qT[0:64, j, j0:j0 + 512], start=True, stop=True)
                    return e.matmul(ps[:, (2 * sg + 1) * 512:(2 * sg + 2) * 512], lhsT=kT[64:128, kb * 128:(kb + 1) * 128],
                                    rhs=qT[64:128, j, j0:j0 + 512], start=True, stop=True)
                P.op(PE, f, reads=[T("kT"), T("qT")], writes=hb(2 * sg, 2))

            def pv_op(s):
                j, kb = steps[s]
                sg = s % 2
                slot = ctr["pt"] % NPT
                ctr["pt"] += 1
                P.op(ACT, lambda e: e.activation(out=PT[slot], in_=ps[:, 2 * sg * 512:(2 * sg + 2) * 512], func=AF.Exp,
                                                 bias=nbias[:], scale=1.0),
                     reads=hb(2 * sg, 2) + [T("nbias"), G2], writes=[T("PT%d" % slot)])

                def f(e):
                    e.matmul(ps[:, 4 * 512:5 * 512], lhsT=vp[:, kb, 0:128], rhs=PT[slot][:, 0:512],
                             start=(kb == 0), stop=(kb == 15))
                    return e.matmul(ps[:, 5 * 512:6 * 512], lhsT=vp[:, kb, 128:256], rhs=PT[slot][:, 512:1024],
                                    start=(kb == 0), stop=(kb == 15))
                P.op(PE, f, reads=[T("vp"), T("PT%d" % slot), G2], writes=hb(4, 2))
                if kb == 15:
                    P.op(DVE, lambda e: e.tensor_copy(ocp[:], ps[:, 4 * 512:6 * 512]), reads=hb(4, 2), writes=[T("ocp")])
                    for hp in range(2):
                        orow = slice(hp * 64, hp * 64 + 64)
                        srow = slice((1 - hp) * 64, (1 - hp) * 64 + 64)
                        cs = slice(hp * 512, (hp + 1) * 512)
                        if j == 3:
                            P.op(ACT, lambda e, orow=orow, srow=srow, cs=cs: e.activation(out=rcp[orow, :], in_=ocp[srow, cs], func=AF.Ln),
                                 reads=[T("ocp")], writes=[T("rcp")])
                            P.op(ACT, lambda e, orow=orow: e.activation(out=rcp[orow, :], in_=rcp[orow, :], func=AF.Exp, scale=-1.0),
                                 reads=[T("rcp")], writes=[T("rcp")])
                        else:
                            P.op(DVE, lambda e, orow=orow, srow=srow, cs=cs: e.reciprocal(rcp[orow, :], ocp[srow, cs]),
                                 reads=[T("ocp")], writes=[T("rcp")])
                        P.op(DVE, lambda e, orow=orow, cs=cs: e.tensor_tensor(
                            out=mixedT[orow, j, :], in0=ocp[orow, cs], in1=rcp[orow, :], op=ALU.mult),
                            reads=[T("ocp"), T("rcp")], writes=[T("mixedT")])

            funits = [(mi, hf, sl) for mi in range(2) for hf in range(2) for sl in range(4)]

            def fourier_micro(m, cn=c):
                u, a = divmod(m, 4)
                mi, hf, sl = funits[u]
                j0 = cn * 512
                if a == 0:
                    dmat = (dftc_d, dfts_d)[mi]
                    slot = ctr["ds"] % NDS
                    ctr["ds"] += 1
                    ctr["cur_ds"] = slot
                    P.dma(SP, lambda e: e.dma_start(
                        out=dsl[slot], in_=dmat[sl * 512:(sl + 1) * 512, j0:j0 + 512].rearrange("(a p) j -> p a j", p=128)),
                        "ds%d" % slot, reads=[G2], writes=[T("dsl%d" % slot)])
                slot = ctr["cur_ds"]
                kb = sl * 4 + a

                def f(e):
                    for ci in range(2):
                        cc = 2 * hf + ci
                        bk = 6 + ci
                        r = e.matmul(ps[:, bk * 512:(bk + 1) * 512], lhsT=U[:, kb, cc * 128:(cc + 1) * 128],
                                     rhs=dsl[slot][:, a, :], start=(kb == 0), stop=(kb == 15))
                    return r
                P.op(PE, f, reads=[T("U"), T("dsl%d" % slot), G2], writes=hb(6, 2))
                if sl == 3 and a == 3:
                    for ci in range(2):
                        cc = 2 * hf + ci
                        bk = 6 + ci
                        ab = mi * 4 + cc
                        P.op(DVE, lambda e, bk=bk, ab=ab: e.tensor_copy(atbt[:, ab, :], ps[:, bk * 512:(bk + 1) * 512]),
                             reads=hb(bk, 1), writes=[T("atbt%d" % ab)])

            qk_op(0)
            qk_op(1)
            nfu = pre_done
            nmic = 4 * len(funits)
            pace = 1 if pre_done == 0 else 2
            for s in range(len(steps)):
                pv_op(s)
                if s + 2 < len(steps):
                    qk_op(s + 2)
                if s >= (6 if pace == 1 else 0) and (s % pace == 0) and nfu < nmic:
                    fourier_micro(nfu)
                    nfu += 1
                if s >= 2 and prev_tail:
                    prev_tail.pop(0)()
            while nfu < nmic:
                fourier_micro(nfu)
                nfu += 1
            while prev_tail:
                prev_tail.pop(0)()
            stage("attn")
            for cc in range(4):
                def f(e, cc=cc):
                    e.matmul(ps[:, cc * 512:(cc + 1) * 512], lhsT=bdm[:, cc, :], rhs=atbt[:, cc, :], start=True, stop=False)
                    return e.matmul(ps[:, cc * 512:(cc + 1) * 512], lhsT=bdm[:, 4 + cc, :], rhs=atbt[:, 4 + cc, :], start=False, stop=True)
                P.op(PE, f, reads=[T("bdm"), T("atbt%d" % cc), T("atbt%d" % (4 + cc))], writes=hb(cc, 1))
                if cc % 2 == 0:
                    P.op(DVE, lambda e, cc=cc: e.tensor_copy(mixedT[:, 4 + cc, :], ps[:, cc * 512:(cc + 1) * 512]),
                         reads=hb(cc, 1), writes=[T("mixedT")])
                else:
                    P.op(ACT, lambda e, cc=cc: e.activation(out=mixedT[:, 4 + cc, :], in_=ps[:, cc * 512:(cc + 1) * 512], func=AF.Copy),
                         reads=hb(cc, 1), writes=[T("mixedT")])

            stage("fourier")
            prefetch = (c + 1 < 4) and stop is None
            npre = [0]
            P.dma(SP, lambda e: e.dma_start(out=xbuf[:], in_=x_d[r0:r0 + 512, :].rearrange("(t p) d -> p t d", p=128)),
                  "xl", writes=[T("xb%d" % t) for t in range(4)])
            for t in range(4):
                pb = (4, 0, 2, 4)[t]

                def f(e, t=t, pb=pb):
                    for m in range(8):
                        for h in range(2):
                            r = e.matmul(ps[:, (pb + h) * 512:(pb + h + 1) * 512], lhsT=mixedT[:, m, t * 128:(t + 1) * 128],
                                         rhs=wout[:, m, h * 512:(h + 1) * 512], start=(m == 0), stop=(m == 7))
                    return r
                P.op(PE, f, reads=[T("mixedT"), T("wout")], writes=hb(pb, 2))
                P.op(DVE, lambda e, t=t, pb=pb: e.tensor_tensor(out=xbuf[:, t, :], in0=ps[:, pb * 512:(pb + 2) * 512],
                                                                 in1=xbuf[:, t, :], op=ALU.add),
                     reads=hb(pb, 2) + [T("xb%d" % t)], writes=[T("xb%d" % t)])
                rms_stats(xbuf[:, t, :], t, [T("xb%d" % t)], "c")
                P.op(DVE, lambda e, t=t: e.tensor_scalar(xn[:, t, :], xbuf[:, t, :], rstd[:, t:t + 1], None, op0=ALU.mult),
                     reads=[T("xb%d" % t), T("rstd%d" % t)], writes=[T("xn%d" % t)])
                if prefetch:
                    for _ in range(4):
                        fourier_micro(npre[0], c + 1)
                        npre[0] += 1
            for kc in range(8):
                transpose_kc(kc, g2, T("g2"), hT, "hT")
                if prefetch:
                    for _ in range(2):
                        fourier_micro(npre[0], c + 1)
                        npre[0] += 1

            stage("outproj")
            for fs in range(8):
                slot = ctr["ws"] % NWS
                ctr["ws"] += 1
                wv = wsl[slot][:, :].rearrange("p (k f) -> p k f", f=512)
                P.dma(SP, lambda e, fs=fs, wv=wv: e.dma_start(
                    out=wv, in_=wup_bf_d[:, fs * 512:(fs + 1) * 512].rearrange("(k p) f -> p k f", p=128)),
                    "ws%d" % slot, reads=[T("wup_bf%d" % i) for i in range(8)], writes=[T("wsl%d" % slot)])
                for fi in range(4):
                    fc = fs * 4 + fi
                    bk = fc % 8

                    def f(e, wv=wv, fi=fi, bk=bk):
                        for kc in range(8):
                            r = e.matmul(ps[:, bk * 512:(bk + 1) * 512], lhsT=wv[:, kc, fi * 128:(fi + 1) * 128],
                                         rhs=hT[:, kc, :], start=(kc == 0), stop=(kc == 7))
                        return r
                    P.op(PE, f, reads=[T("wsl%d" % slot)] + [T("hT%d" % kc) for kc in range(8)], writes=hb(bk, 1))
                    zs = ctr["z1"] % 2
                    ctr["z1"] += 1
                    P.op(ACT, lambda e, bk=bk, zs=zs: e.activation(out=z1[zs][:], in_=ps[:, bk * 512:(bk + 1) * 512], func=AF.Relu),
                         reads=hb(bk, 1), writes=[T("z1_%d" % zs)])
                    P.op(POOL, lambda e, fc=fc, zs=zs: e.tensor_tensor(out=zT[:, fc, :], in0=z1[zs][:], in1=z1[zs][:], op=ALU.mult),
                         reads=[T("z1_%d" % zs), G1], writes=[T("zT")])

            stage("mlpup")
            for dsb in range(8):
                slot = ctr["ws"] % NWS
                ctr["ws"] += 1
                wv = wsl[slot][:, :].rearrange("p (a n) -> p a n", n=1024)
                P.dma(SP, lambda e, dsb=dsb, wv=wv: e.dma_start(
                    out=wv, in_=wdown_bf_d[dsb * 512:(dsb + 1) * 512, :].rearrange("(a p) n -> p a n", p=128)),
                    "ws%d" % slot, reads=[T("wdown_bf%d" % dsb)], writes=[T("wsl%d" % slot)])
                for t in range(4):
                    for h in range(2):
                        bk = t * 2 + h

                        def f(e, wv=wv, dsb=dsb, t=t, h=h, bk=bk):
                            for a in range(4):
                                fc = dsb * 4 + a
                                r = e.matmul(ps[:, bk * 512:(bk + 1) * 512], lhsT=zT[:, fc, t * 128:(t + 1) * 128],
                                             rhs=wv[:, a, h * 512:(h + 1) * 512], start=(fc == 0), stop=(fc == 31))
                            return r
                        P.op(PE, f, reads=[T("wsl%d" % slot), T("zT"), G1], writes=hb(bk, 1))
            for t in range(4):
                P.op(DVE, lambda e, t=t: e.tensor_tensor(out=xbuf[:, t, :], in0=ps[:, 2 * t * 512:(2 * t + 2) * 512],
                                                          in1=xbuf[:, t, :], op=ALU.add),
                     reads=hb(2 * t, 2) + [T("xb%d" % t)], writes=[T("xb%d" % t)])
            tail = []
            for t in range(4):
                def stats(t=t):
                    rms_stats(xbuf[:, t, :], t, [T("xb%d" % t)], "f")

                def scale(t=t):
                    P.op(DVE, lambda e: e.scalar_tensor_tensor(out=xbuf[:, t, :], in0=xbuf[:, t, :], scalar=rstd[:, t:t + 1],
                                                               in1=gf_bc[:], op0=ALU.mult, op1=ALU.mult),
                         reads=[T("xb%d" % t), T("rstd%d" % t), T("gf")], writes=[T("xb%d" % t)])
                def store(t=t):
                    P.dma(SP, lambda e: e.dma_start(out=out_d[r0 + t * 128:r0 + (t + 1) * 128, :], in_=xbuf[:, t, :]),
                          "os%d" % t, reads=[T("xb%d" % t)], final=True)
                tail.append(stats)
                tail.append(scale)
                tail.append(store)
            if stop == "chunk":
                for f_ in tail:
                    f_()
                tail = []
            stage("chunk")
            return tail, npre[0]

        def stage(name):
            if stop == name:
                raise _Stop()

        try:
            stage("setup")
            for b in range(nseq):
                fence()
                phase_a(b)
                while b == 0 and late_casts:
                    cast_piece(*late_casts.pop(0))
                stage("phaseA")
                fence()
                tail, npre_ = [], 0
                for c in range(4):
                    tail, npre_ = chunk(b, c, stage, tail, npre_)
                for f_ in tail:
                    f_()
        except _Stop:
            pass
        if stop is not None:
            for e_ in ENGINES:
                for o_ in P.ops[e_]:
                    if o_.is_dma and o_ not in P.final_waits:
                        P.final_waits.append(o_)
        if dumps:
            allb = list(B.values()) + HB
            loc = dict(locals())
            for nm in dumps:
                ap = loc[nm]
                shp = list(ap.shape)
                dd = nc.dram_tensor("dbg_" + nm, shp, ap.dtype, kind="ExternalOutput").ap()
                src = ap if isinstance(ap, bass.AP) else ap[:]
                P.dma(SP, lambda e, dd=dd, src=src: e.dma_start(out=dd, in_=src), "dbg_" + nm, reads=allb, final=True)
        P.emit(nc, st)
    return nc


def _constants():
    bf = ml_dtypes.bfloat16
    k = np.arange(SEQ, dtype=np.int64)
    kj = (k[:, None] * k[None, :]) % SEQ
    ang = kj.astype(np.float64) * (2.0 * np.pi / SEQ)
    dftc = np.cos(ang).astype(np.float32).astype(bf)
    dfts = np.sin(ang).astype(np.float32).astype(bf)
    c = np.arange(64, dtype=np.int64)
    cang = ((c[:, None] * c[None, :]) % 64).astype(np.float64) * (2.0 * np.pi / 64)
    norm = 1.0 / np.sqrt(SEQ * 64.0)
    cc = np.cos(cang) * norm
    sc = -np.sin(cang) * norm
    z = np.zeros((64, 64))
    ccbd = np.block([[cc, z], [z, cc]]).astype(np.float32).astype(bf)
    scbd = np.block([[sc, z], [z, sc]]).astype(np.float32).astype(bf)
    s = np.arange(SEQ)
    row = (s // 64).astype(np.float32)
    col = (s % 64).astype(np.float32)
    inv = (np.float32(10000.0) ** (-np.arange(0, 32, 2, dtype=np.float32) / np.float32(32))).astype(np.float32)
    ra = row[:, None] * inv[None, :]
    ca = col[:, None] * inv[None, :]
    ropec = np.concatenate([np.cos(ra), np.cos(ra), np.cos(ca), np.cos(ca)], axis=1).astype(np.float32)
    ropes = np.concatenate([-np.sin(ra), np.sin(ra), -np.sin(ca), np.sin(ca)], axis=1).astype(np.float32)
    return dftc, dfts, ccbd, scbd, ropec, ropes


def _swap(g):
    return np.concatenate([g[16:32], g[0:16], g[48:64], g[32:48]])


_CACHE = {}


def kernel(x, mix_norm_g, w_in, q_norm_g, k_norm_g, w_fourier, w_out,
           mlp_norm_g, w_up, w_down, final_norm_g):
    f32 = np.float32
    x = np.asarray(x, f32)
    w_in = np.asarray(w_in, f32)
    w_out = np.asarray(w_out, f32)
    if "nc" not in _CACHE:
        _CACHE["nc"] = build_nc()
        _CACHE["const"] = _constants()
    nc = _CACHE["nc"]
    dftc, dfts, ccbd, scbd, ropec, ropes = _CACHE["const"]
    qcols = np.concatenate([np.concatenate([np.arange(j * 64, (j + 1) * 64), np.arange((j + 4) * 64, (j + 5) * 64)])
                            for j in range(4)])
    cols = np.concatenate([qcols, np.arange(512, 1280)])
    w_in_p = np.ascontiguousarray(w_in[:, cols])
    rows = np.concatenate([qcols, np.arange(512, 1024)])
    w_out_p = np.ascontiguousarray(w_out[rows, :])
    gq = np.asarray(q_norm_g, f32)
    gk = np.asarray(k_norm_g, f32)
    g_qk = np.concatenate([gq, _swap(gq), gk, _swap(gk)]).reshape(1, 256).astype(f32)
    common = {
        "w_in": w_in_p, "w_out": w_out_p,
        "w_up": np.ascontiguousarray(np.asarray(w_up, f32)),
        "w_down": np.ascontiguousarray(np.asarray(w_down, f32)),
        "wf": np.ascontiguousarray(np.asarray(w_fourier, f32).reshape(512, 64)),
        "g_mix": np.asarray(mix_norm_g, f32).reshape(1, DM),
        "g_mlp": np.asarray(mlp_norm_g, f32).reshape(1, DM),
        "g_fin": np.asarray(final_norm_g, f32).reshape(1, DM),
        "g_qk": g_qk,
        "dftc": dftc, "dfts": dfts, "ropec": ropec, "ropes": ropes, "ccbd": ccbd, "scbd": scbd,
    }
    in_maps = []
    for c in range(N_CORES):
        m = dict(common)
        m["x"] = np.ascontiguousarray(x[c * NSEQ:(c + 1) * NSEQ].reshape(NSEQ * SEQ, DM))
        in_maps.append(m)
    res = run_bass_kernel_spmd(nc, in_maps, core_ids=list(range(N_CORES)))
    outs = [np.asarray(r["out"], f32).reshape(NSEQ, SEQ, DM) for r in res.results]
    return np.concatenate(outs, axis=0)
```

```python
from contextlib import ExitStack
import numpy as np
import ml_dtypes
import concourse.bass as bass
import concourse.mybir as mybir
from concourse.bass_utils import run_bass_kernel_spmd

F32 = mybir.dt.float32
BF16 = mybir.dt.bfloat16
ALU = mybir.AluOpType
AF = mybir.ActivationFunctionType
AX = mybir.AxisListType

PE, ACT, DVE, POOL, SP = "tensor", "scalar", "vector", "gpsimd", "sync"
ENGINES = (PE, ACT, DVE, POOL, SP)

N_CORES = 8
SEQ = 2048
DM = 1024
DFF = 4096
NSEQ = 2
EPS = 1e-6


class Buf:
    __slots__ = ("name", "writer", "readers")

    def __init__(self, name):
        self.name = name
        self.writer = None
        self.readers = []


class Op:
    __slots__ = ("eng", "fn", "deps", "is_dma", "key", "signal", "tok")

    def __init__(self, eng, fn, is_dma=False, key=None):
        self.eng = eng
        self.fn = fn
        self.deps = {}
        self.is_dma = is_dma
        self.key = key
        self.signal = False
        self.tok = None


class Prog:
    def __init__(self):
        self.ops = {e: [] for e in ENGINES}
        self.final_waits = []

    def _track(self, op, reads, writes):
        deps = []
        for b in reads:
            if b.writer is not None:
                deps.append((b.writer, "RAW"))
        for b in writes:
            if b.writer is not None:
                deps.append((b.writer, "WAW"))
            for r in b.readers:
                deps.append((r, "WAR"))
        for d, kind in deps:
            if d is op:
                continue
            if d.eng == op.eng and not d.is_dma and not op.is_dma and op.eng != POOL:
                if kind == "WAW" or op.eng == PE:
                    continue
            if id(d) not in op.deps:
                op.deps[id(d)] = d
                d.signal = True
        for b in reads:
            b.readers.append(op)
        for b in writes:
            b.writer = op
            b.readers = []

    def op(self, eng, fn, reads=(), writes=()):
        o = Op(eng, fn)
        self._track(o, reads, writes)
        self.ops[eng].append(o)
        return o

    def dma(self, eng, fn, key, reads=(), writes=(), final=False):
        o = Op(eng, fn, is_dma=True, key=key)
        o.signal = True
        self._track(o, reads, writes)
        self.ops[eng].append(o)
        if final:
            self.final_waits.append(o)
        return o

    def emit(self, nc, stack):
        keycount = {}
        for e in ENGINES:
            for i, o in enumerate(self.ops[e]):
                if o.is_dma:
                    keycount[o.key] = keycount.get(o.key, 0) + 16
                    o.tok = ("d_" + o.key, keycount[o.key])
                else:
                    o.tok = ("e_" + e, i + 1)

        def plan(e):
            waited = {}
            out = []
            for o in self.ops[e]:
                need = {}
                for d in o.deps.values():
                    sn, v = d.tok
                    if need.get(sn, 0) < v:
                        need[sn] = v
                ws = []
                for sn, v in need.items():
                    if waited.get(sn, 0) >= v:
                        continue
                    ws.append((sn, v))
                    waited[sn] = v
                out.append(ws)
            fin = {}
            if e == SP:
                for d in self.final_waits:
                    sn, v = d.tok
                    if fin.get(sn, 0) < v:
                        fin[sn] = v
            return out, list(fin.items())

        plans = {e: plan(e) for e in ENGINES}
        used = {}
        for e in ENGINES:
            ws_list, fin = plans[e]
            for ws in ws_list:
                for sn, v in ws:
                    used.setdefault(sn, set()).add(v)
            for sn, v in fin:
                used.setdefault(sn, set()).add(v)
        final_val = {}
        for e in ENGINES:
            sn = "e_" + e
            vals = sorted(used.get(sn, ()))
            final_val[sn] = {v: r + 1 for r, v in enumerate(vals)}
        sems = {}
        for sn in used:
            sems[sn] = stack.enter_context(nc.semaphore("s_" + sn))
        for e in ENGINES:
            for o in self.ops[e]:
                if o.is_dma and o.tok[0] not in sems:
                    sems[o.tok[0]] = stack.enter_context(nc.semaphore("s_" + o.tok[0]))
        block = stack.enter_context(nc.Block())
        prog = self

        def xl(sn, v):
            return final_val[sn][v] if sn in final_val and sn.startswith("e_") else v

        def make(e):
            def body(eng):
                ws_list, fin = plans[e]
                for o, ws in zip(prog.ops[e], ws_list):
                    for sn, v in ws:
                        eng.wait_ge(sems[sn], xl(sn, v))
                    ins = o.fn(eng)
                    sn, v = o.tok
                    if o.is_dma:
                        ins.then_inc(sems[sn], 16)
                    elif v in final_val.get(sn, ()):
                        ins.then_inc(sems[sn], 1)
                for sn, v in fin:
                    eng.wait_ge(sems[sn], xl(sn, v))
            return body

        block.tensor(make(PE))
        block.scalar(make(ACT))
        block.vector(make(DVE))
        block.gpsimd(make(POOL))
        block.sync(make(SP))
        self.nsems = len(sems)
        self.nsignals = {sn: len(m) for sn, m in final_val.items()}


class _Stop(Exception):
    pass


_DBG = {"nolate": False}


def build_nc(nseq=NSEQ, stop=None, dumps=()):
    nc = bass.Bass("TRN2", target_bir_lowering=False)
    NTOK = nseq * SEQ

    def din(name, shape, dt=F32):
        return nc.dram_tensor(name, shape, dt, kind="ExternalInput").ap()

    x_d = din("x", [NTOK, DM])
    w_in_d = din("w_in", [DM, 1280])
    w_out_d = din("w_out", [DM, DM])
    w_up_d = din("w_up", [DM, DFF])
    w_down_d = din("w_down", [DFF, DM])
    wf_d = din("wf", [512, 64])
    gmix_d = din("g_mix", [1, DM])
    gmlp_d = din("g_mlp", [1, DM])
    gfin_d = din("g_fin", [1, DM])
    gqk_d = din("g_qk", [1, 256])
    dftc_d = din("dftc", [SEQ, SEQ], BF16)
    dfts_d = din("dfts", [SEQ, SEQ], BF16)
    ropec_d = din("ropec", [SEQ, 64])
    ropes_d = din("ropes", [SEQ, 64])
    ccbd_d = din("ccbd", [128, 128], BF16)
    scbd_d = din("scbd", [128, 128], BF16)
    out_d = nc.dram_tensor("out", [NTOK, DM], F32, kind="ExternalOutput").ap()
    win_bf_d = nc.dram_tensor("win_bf", [DM, 1280], BF16, kind="Internal").ap()
    wout_bf_d = nc.dram_tensor("wout_bf", [DM, DM], BF16, kind="Internal").ap()
    wup_bf_d = nc.dram_tensor("wup_bf", [DM, DFF], BF16, kind="Internal").ap()
    wdown_bf_d = nc.dram_tensor("wdown_bf", [DFF, DM], BF16, kind="Internal").ap()

    P = Prog()
    st = ExitStack()
    with st:
        def sb(name, shape, dt):
            return st.enter_context(nc.sbuf_tensor(name, shape, dt))

        ident = sb("ident", [128, 128], BF16)
        g1 = sb("g1", [128, 8], F32)
        g2 = sb("g2", [128, 8], F32)
        gf_bc = sb("gf_bc", [128, DM], F32)
        gqk = sb("gqk", [128, 4, 64], F32)
        gmax = sb("gmax", [128, 4], F32)
        nbias = sb("nbias", [128, 1], F32)
        epst = sb("epst", [128, 1], F32)
        ccbd = sb("ccbd_sb", [128, 128], BF16)
        scbd = sb("scbd_sb", [128, 128], BF16)
        bdm = sb("bdm", [128, 8, 128], BF16)
        wout = sb("wout_sb", [128, 8, DM], BF16)
        vp = sb("vp", [128, 16, 256], BF16)
        qT = sb("qT", [128, 4, SEQ], BF16)
        kT = sb("kT", [128, SEQ], BF16)
        ocp = sb("ocp", [128, 1024], F32)
        U = sb("U", [128, 16, 512], BF16)
        r1 = sb("r1", [128, 32 * 512], BF16)
        win = r1[:, 0:8 * 1280].rearrange("p (k n) -> p k n", k=8)
        zT = r1[:, :].rearrange("p (f t) -> p f t", t=512)
        hT_b = r1[:, 10240:14336].rearrange("p (k t) -> p k t", t=512)
        xbuf = sb("xbuf", [128, 4, DM], F32)
        xn = sb("xn", [128, 4, DM], BF16)
        hT = sb("hT", [128, 8, 512], BF16)
        junk = sb("junk", [128, DM], BF16)
        fz = sb("fz", [128, 8], BF16)
        ss = sb("ss", [128, 4], F32)
        lnv = sb("lnv", [128, 4], F32)
        rstd = sb("rstd", [128, 4], F32)
        r2 = sb("r2", [128, 12288], BF16)
        r2f = r2[:, :].bitcast(F32)
        rC = r2f[:, 0:256].rearrange("p (t d) -> p t d", d=64)
        rS = r2f[:, 256:512].rearrange("p (t d) -> p t d", d=64)
        Cq = r2f[:, 512:768].rearrange("p (t d) -> p t d", d=64)
        Sq = r2f[:, 768:1024].rearrange("p (t d) -> p t d", d=64)
        Ck = r2f[:, 1024:1280].rearrange("p (t d) -> p t d", d=64)
        Sk = r2f[:, 1280:1536].rearrange("p (t d) -> p t d", d=64)
        SETS = []
        for si in range(2):
            base = 1536 + si * 2240
            SETS.append(dict(
                nq=r2f[:, base:base + 640],
                t1=r2f[:, base + 640:base + 1280],
                t2=r2f[:, base + 1280:base + 1920],
                qk=r2[:, 2 * (base + 1920):2 * (base + 1920) + 640],
                ssq=r2f[:, 6016 + si * 48:6016 + si * 48 + 10],
                lnq=r2f[:, 6016 + si * 48 + 16:6016 + si * 48 + 26],
                rs10=r2f[:, 6016 + si * 48 + 32:6016 + si * 48 + 42],
            ))
        NPT = 4
        PT = [r2[:, i * 1024:(i + 1) * 1024] for i in range(NPT)]
        NDS = 4
        dsl = [r2[:, 4096 + i * 2048: 4096 + (i + 1) * 2048].rearrange("p (a j) -> p a j", j=512)
               for i in range(NDS)]
        mixedT = sb("mixedT", [128, 8, 512], BF16)
        atbt = sb("atbt", [128, 8, 512], BF16)
        wfz = atbt[:, 0:2, :].rearrange("p a t -> p (a t)").bitcast(F32).rearrange("p (i d) -> p i d", d=128)
        wfb = atbt[:, 2, :].rearrange("p (i d) -> p i d", d=128)
        gtmp = atbt[:, 3, :].bitcast(F32).rearrange("p (i d) -> p i d", d=64)
        identz = atbt[:, 4, 0:128]
        NWS = 3
        wsl = [sb("wsl%d" % i, [128, 4096], BF16) for i in range(NWS)]
        rcp = sb("rcp", [128, 512], F32)
        z1 = [sb("z1_%d" % i, [128, 512], BF16) for i in range(2)]

        ps = st.enter_context(nc.psum_tensor("ps", [128, 4096], F32))
        psb = ps[:, :].bitcast(BF16)
        HB = [Buf("bank%d" % i) for i in range(8)]

        def hb(i, n=1):
            return HB[i:i + n]

        B = {}

        def T(name):
            if name not in B:
                B[name] = Buf(name)
            return B[name]

        G1 = T("guard_r1")
        G2 = T("guard_r2")

        P.op(POOL, lambda e: e.memset(identz, 0.0), writes=[T("atbt4")])
        P.op(POOL, lambda e: e.affine_select(out=ident[:], in_=identz, compare_op=ALU.not_equal, fill=1.0,
                                             base=0, pattern=[[-1, 128]], channel_multiplier=1),
             reads=[T("atbt4")], writes=[T("ident")])
        P.op(POOL, lambda e: e.memset(vp[:, :, 64:192], 1.0), writes=[T("vp_ones")])
        P.op(POOL, lambda e: e.memset(epst[:], EPS), writes=[T("epst")])
        P.op(POOL, lambda e: e.memset(ocp[:], 0.0), writes=[T("ocp")])
        P.op(POOL, lambda e: e.memset(wfz, 0.0), writes=[T("atbt0"), T("atbt1")])

        def cast_piece(dst, src, r0, r1_, key, tokname):
            P.dma(POOL, lambda e: e.dma_start(out=dst[r0:r1_, :], in_=src[r0:r1_, :]), key, writes=[T(tokname)])
        for i in range(2):
            cast_piece(win_bf_d, w_in_d, i * 512, (i + 1) * 512, "c_win%d" % i, "win_bf%d" % i)
        for i in range(2):
            cast_piece(wout_bf_d, w_out_d, i * 512, (i + 1) * 512, "c_wout%d" % i, "wout_bf%d" % i)
        late_casts = []
        wup_bf_v = wup_bf_d.rearrange("r (a n) -> (r a) n", n=1024)
        w_up_v = w_up_d.rearrange("r (a n) -> (r a) n", n=1024)
        for i in range(8):
            late_casts.append((wup_bf_v, w_up_v, i * 512, (i + 1) * 512, "c_wup%d" % i, "wup_bf%d" % i))
        for i in range(8):
            late_casts.append((wdown_bf_d, w_down_d, i * 512, (i + 1) * 512, "c_wdn%d" % i, "wdown_bf%d" % i))
        if _DBG["nolate"]:
            late_casts = []

        P.dma(SP, lambda e: e.dma_start(out=g1[:], in_=gmix_d.rearrange("o (k p) -> p (o k)", p=128),
                                        allow_slow_non_contiguous=True), "k_g1", writes=[T("g1")])
        P.dma(SP, lambda e: e.dma_start(out=gqk[:].rearrange("p a d -> p (a d)"), in_=gqk_d.partition_broadcast(128)),
              "k_gqk", writes=[T("gqk")])
        P.dma(SP, lambda e: e.dma_start(out=ccbd[:], in_=ccbd_d), "k_ccbd", writes=[T("ccbd")])
        P.dma(SP, lambda e: e.dma_start(out=scbd[:], in_=scbd_d), "k_scbd", writes=[T("scbd")])
        for i in range(4):
            P.dma(SP, lambda e, i=i: e.dma_start(out=wfz[0:64, i, 0:64], in_=wf_d[(2 * i) * 64:(2 * i + 1) * 64, :]),
                  "k_wf%d" % (2 * i), reads=[T("atbt0"), T("atbt1")], writes=[T("wfz_a%d" % i)])
            P.dma(SP, lambda e, i=i: e.dma_start(out=wfz[64:128, i, 64:128], in_=wf_d[(2 * i + 1) * 64:(2 * i + 2) * 64, :]),
                  "k_wf%d" % (2 * i + 1), reads=[T("atbt0"), T("atbt1")], writes=[T("wfz_b%d" % i)])
        P.dma(SP, lambda e: e.dma_start(out=g2[:], in_=gmlp_d.rearrange("o (k p) -> p (o k)", p=128),
                                        allow_slow_non_contiguous=True), "k_g2", writes=[T("g2")])
        P.dma(SP, lambda e: e.dma_start(out=gf_bc[:], in_=gfin_d.partition_broadcast(128)), "k_gf", writes=[T("gf")])

        P.op(DVE, lambda e: e.tensor_scalar(gtmp, gqk[:], -1.0, None, op0=ALU.mult), reads=[T("gqk")], writes=[T("atbt3")])
        P.op(DVE, lambda e: e.tensor_tensor(out=gtmp, in0=gtmp, in1=gqk[:], op=ALU.max), reads=[T("atbt3"), T("gqk")], writes=[T("atbt3")])
        P.op(DVE, lambda e: e.tensor_reduce(out=gmax[:], in_=gtmp, axis=AX.X, op=ALU.max), reads=[T("atbt3")], writes=[T("gmax")])
        P.op(DVE, lambda e: e.tensor_tensor(out=nbias[:], in0=gmax[:, 0:1], in1=gmax[:, 2:3], op=ALU.mult), reads=[T("gmax")], writes=[T("nbias")])
        P.op(DVE, lambda e: e.tensor_scalar(nbias[:], nbias[:], -8.0, None, op0=ALU.mult), reads=[T("nbias")], writes=[T("nbias")])

        gqs = sb("gqs", [128, 4, 64], F32)

        def mk_gqs(e):
            e.tensor_scalar(gqs[:, 0:2, :], gqk[:, 0:2, :], 0.125, None, op0=ALU.mult)
            return e.tensor_copy(gqs[:, 2:4, :], gqk[:, 2:4, :])
        P.op(DVE, mk_gqs, reads=[T("gqk")], writes=[T("gqs")])

        P.op(DVE, lambda e: e.tensor_copy(wfb, wfz),
             reads=[T("atbt0"), T("atbt1")] + [T("wfz_a%d" % i) for i in range(4)] + [T("wfz_b%d" % i) for i in range(4)], writes=[T("atbt2")])

        def bd_mm(e):
            for i in range(4):
                e.matmul(ps[:, i * 128:(i + 1) * 128], lhsT=ccbd[:, :], rhs=wfb[:, i, :], start=True, stop=True)
            for i in range(4):
                r = e.matmul(ps[:, 512 + i * 128:512 + (i + 1) * 128], lhsT=scbd[:, :], rhs=wfb[:, i, :], start=True, stop=True)
            return r
        P.op(PE, bd_mm, reads=[T("ccbd"), T("scbd"), T("atbt2")], writes=hb(0, 2))
        P.op(DVE, lambda e: e.tensor_copy(bdm[:].rearrange("p a d -> p (a d)"), ps[:, 0:1024]), reads=hb(0, 2), writes=[T("bdm")])

        P.dma(SP, lambda e: e.dma_start(out=wout[:], in_=wout_bf_d.rearrange("(k p) n -> p k n", p=128)), "k_wout",
              reads=[T("wout_bf0"), T("wout_bf1")], writes=[T("wout")])

        def fence():
            P.op(POOL, lambda e: e.memset(fz[:], 0.0), writes=[G1, G2, T("fz")])

        def rms_stats(src_ap, col, src_toks, tag):
            P.op(ACT, lambda e: e.activation(out=junk[:], in_=src_ap, func=AF.Square, accum_out=ss[:, col:col + 1]),
                 reads=src_toks, writes=[T("junk"), T("ss%d" % col)])
            P.op(ACT, lambda e: e.activation(out=lnv[:, col:col + 1], in_=ss[:, col:col + 1], func=AF.Ln,
                                             bias=epst[:], scale=1.0 / DM),
                 reads=[T("ss%d" % col), T("epst")], writes=[T("lnv%d" % col)])
            P.op(ACT, lambda e: e.activation(out=rstd[:, col:col + 1], in_=lnv[:, col:col + 1], func=AF.Exp, scale=-0.5),
                 reads=[T("lnv%d" % col)], writes=[T("rstd%d" % col)])

        def transpose_kc(kc, gvec, gtok, hdst, htag, extra=()):
            h = kc % 2

            def tr(e):
                for t in range(4):
                    r = e.transpose(psb[:, h * 1024 + t * 128: h * 1024 + (t + 1) * 128],
                                    xn[:, t, kc * 128:(kc + 1) * 128], ident[:])
                return r
            P.op(PE, tr, reads=[T("xn%d" % t) for t in range(4)] + [T("ident")], writes=[HB[h]])
            P.op(DVE, lambda e: e.tensor_scalar(hdst[:, kc, :], psb[:, h * 1024:h * 1024 + 512],
                                                gvec[:, kc:kc + 1], None, op0=ALU.mult),
                 reads=[HB[h], gtok] + list(extra), writes=[T("%s%d" % (htag, kc))])

        def transposes_to_hT(gvec, gtok):
            for kc in range(8):
                transpose_kc(kc, gvec, gtok, hT, "hT")

        def phase_a(b):
            P.dma(SP, lambda e: e.dma_start(out=win, in_=win_bf_d.rearrange("(k p) n -> p k n", p=128)), "k_win",
                  reads=[T("win_bf0"), T("win_bf1"), G1], writes=[T("win")])
            hbufs = [(hT, "hT", ()), (hT_b, "hTb", (G1,))]

            def head_load(g):
                r0 = b * SEQ + g * 512
                P.dma(SP, lambda e: e.dma_start(out=xbuf[:], in_=x_d[r0:r0 + 512, :].rearrange("(t p) d -> p t d", p=128)),
                      "xl", writes=[T("xb%d" % t) for t in range(4)])

            def head_tile(t):
                rms_stats(xbuf[:, t, :], t, [T("xb%d" % t)], "a")
                P.op(DVE, lambda e: e.tensor_scalar(xn[:, t, :], xbuf[:, t, :], rstd[:, t:t + 1], None, op0=ALU.mult),
                     reads=[T("xb%d" % t), T("rstd%d" % t)], writes=[T("xn%d" % t)])

            def tables(g):
                s0 = g * 512
                P.dma(SP, lambda e: e.dma_start(out=rC, in_=ropec_d[s0:s0 + 512, :].rearrange("(t p) d -> p t d", p=128)),
                      "rc", reads=[G2], writes=[T("rC")])
                P.dma(SP, lambda e: e.dma_start(out=rS, in_=ropes_d[s0:s0 + 512, :].rearrange("(t p) d -> p t d", p=128)),
                      "rs", reads=[G2], writes=[T("rS")])

                def bc(i):
                    return gqs[:, i, :].unsqueeze(1).to_broadcast([128, 4, 64])
                P.op(POOL, lambda e: e.tensor_tensor(out=Cq, in0=rC, in1=bc(0), op=ALU.mult),
                     reads=[T("rC"), T("gqs"), G2], writes=[T("Cq")])
                P.op(POOL, lambda e: e.tensor_tensor(out=Sq, in0=rS, in1=bc(1), op=ALU.mult),
                     reads=[T("rS"), T("gqs"), G2], writes=[T("Sq")])
                P.op(POOL, lambda e: e.tensor_tensor(out=Ck, in0=rC, in1=bc(2), op=ALU.mult),
                     reads=[T("rC"), T("gqs"), G2], writes=[T("Ck")])
                P.op(POOL, lambda e: e.tensor_tensor(out=Sk, in0=rS, in1=bc(3), op=ALU.mult),
                     reads=[T("rS"), T("gqs"), G2], writes=[T("Sk")])

            def inproj(g, t):
                hsrc, htag, hextra = hbufs[g % 2]
                pb = 2 + 3 * (t % 2)

                def f(e):
                    for kc in range(8):
                        for (c0, c1) in ((0, 512), (512, 1024), (1024, 1280)):
                            r = e.matmul(ps[:, pb * 512 + c0: pb * 512 + c1], lhsT=hsrc[:, kc, t * 128:(t + 1) * 128],
                                         rhs=win[:, kc, c0:c1], start=(kc == 0), stop=(kc == 7))
                    return r
                P.op(PE, f, reads=[T("%s%d" % (htag, kc)) for kc in range(8)] + [T("win"), G1], writes=hb(pb, 3))

            def post(g, t):
                TT = g * 4 + t
                pb = 2 + 3 * (t % 2)
                pin = ps[:, pb * 512: pb * 512 + 1280]
                S_ = SETS[TT % 2]
                sx = "%d" % (TT % 2)
                nq, t1, t2, qk_tm, ssq, lnq, rs10 = S_["nq"], S_["t1"], S_["t2"], S_["qk"], S_["ssq"], S_["lnq"], S_["rs10"]
                sq = t1
                P.op(ACT, lambda e: e.activation(out=sq, in_=pin[:, 0:640], func=AF.Square),
                     reads=hb(pb, 2) + [G2], writes=[T("t1" + sx)])
                P.op(DVE, lambda e: e.tensor_reduce(out=ssq, in_=sq.rearrange("p (h d) -> p h d", d=64), axis=AX.X, op=ALU.add),
                     reads=[T("t1" + sx), G2], writes=[T("ssq" + sx)])
                P.op(ACT, lambda e: e.activation(out=lnq, in_=ssq, func=AF.Ln, bias=epst[:], scale=1.0 / 64),
                     reads=[T("ssq" + sx), T("epst"), G2], writes=[T("lnq" + sx)])
                P.op(ACT, lambda e: e.activation(out=rs10, in_=lnq, func=AF.Exp, scale=-0.5),
                     reads=[T("lnq" + sx), G2], writes=[T("rs10" + sx)])
                P.op(DVE, lambda e: e.tensor_tensor(out=nq.rearrange("p (h d) -> p h d", d=64),
                                                    in0=pin[:, 0:640].rearrange("p (h d) -> p h d", d=64),
                                                    in1=rs10.unsqueeze(2).to_broadcast([128, 10, 64]), op=ALU.mult),
                     reads=hb(pb, 2) + [T("rs10" + sx), G2], writes=[T("nq" + sx)])

                P.op(ACT, lambda e: e.activation(
                    out=vp[:, TT, :].rearrange("p (a d) -> p a d", d=64)[:, 0:4:3, :],
                    in_=pin[:, 640:768].rearrange("p (a d) -> p a d", d=64), func=AF.Copy),
                    reads=hb(pb, 2) + [T("vp_ones")], writes=[T("vp")])
                P.op(ACT, lambda e: e.activation(out=U[:, TT, :], in_=pin[:, 768:1280], func=AF.Copy),
                     reads=hb(pb + 1, 2), writes=[T("U")])

                def mul_c(e):
                    e.tensor_tensor(out=t1[:, 0:512].rearrange("p (h d) -> p h d", d=64),
                                    in0=nq[:, 0:512].rearrange("p (h d) -> p h d", d=64),
                                    in1=Cq[:, t, :].unsqueeze(1).to_broadcast([128, 8, 64]), op=ALU.mult)
                    return e.tensor_tensor(out=t1[:, 512:640].rearrange("p (h d) -> p h d", d=64),
                                           in0=nq[:, 512:640].rearrange("p (h d) -> p h d", d=64),
                                           in1=Ck[:, t, :].unsqueeze(1).to_broadcast([128, 2, 64]), op=ALU.mult)
                P.op(DVE, mul_c, reads=[T("nq" + sx), T("Cq"), T("Ck"), T("ssq" + sx), G2], writes=[T("t1" + sx)])

                def mul_s(e):
                    r = None
                    for (c0, c1, nh, tab) in ((0, 512, 8, Sq), (512, 640, 2, Sk)):
                        av = nq[:, c0:c1].rearrange("p (h x f d) -> p h x f d", x=2, f=2, d=16)
                        ov = t2[:, c0:c1].rearrange("p (h x f d) -> p h x f d", x=2, f=2, d=16)
                        tv = tab[:, t, :].rearrange("p (x f d) -> p x f d", x=2, f=2)
                        e.tensor_tensor(out=ov[:, :, :, 0, :], in0=av[:, :, :, 1, :],
                                        in1=tv[:, :, 0, :].unsqueeze(1).to_broadcast([128, nh, 2, 16]), op=ALU.mult)
                        r = e.tensor_tensor(out=ov[:, :, :, 1, :], in0=av[:, :, :, 0, :],
                                            in1=tv[:, :, 1, :].unsqueeze(1).to_broadcast([128, nh, 2, 16]), op=ALU.mult)
                    return r
                P.op(POOL, mul_s, reads=[T("nq" + sx), T("Sq"), T("Sk"), G2], writes=[T("t2" + sx)])

            def post_add(g, t):
                TT = g * 4 + t
                S_ = SETS[TT % 2]
                sx = "%d" % (TT % 2)
                t1, t2, qk_tm = S_["t1"], S_["t2"], S_["qk"]
                P.op(DVE, lambda e: e.tensor_tensor(out=qk_tm, in0=t1, in1=t2, op=ALU.add),
                     reads=[T("t1" + sx), T("t2" + sx), G2], writes=[T("qk" + sx)])


            def post_b(g, t):
                TT = g * 4 + t
                sx = "%d" % (TT % 2)
                qk_tm = SETS[TT % 2]["qk"]

                def trqk(e):
                    for i in range(5):
                        r = e.transpose(psb[:, 1024 + i * 128: 1024 + (i + 1) * 128], qk_tm[:, i * 128:(i + 1) * 128], ident[:])
                    return r
                P.op(PE, trqk, reads=[T("qk" + sx), T("ident"), G2], writes=hb(1, 1))
                P.op(ACT, lambda e: e.activation(out=qT[:, :, TT * 128:(TT + 1) * 128],
                                                 in_=psb[:, 1024:1536].rearrange("p (j s) -> p j s", s=128), func=AF.Copy),
                     reads=hb(1, 1), writes=[T("qT")])

                def kcopy(e):
                    e.activation(out=kT[0:64, TT * 128:(TT + 1) * 128], in_=psb[0:64, 1536:1664], func=AF.Copy)
                    return e.activation(out=kT[64:128, TT * 128:(TT + 1) * 128], in_=psb[64:128, 1536:1664], func=AF.Copy)
                P.op(ACT, kcopy, reads=hb(1, 1), writes=[T("kT")])

            head_load(0)
            for t in range(4):
                head_tile(t)
            for kc in range(8):
                transpose_kc(kc, g1, T("g1"), *hbufs[0][:2], extra=hbufs[0][2])
            tiles = [(g, t) for g in range(4) for t in range(4)]
            for i, (g, t) in enumerate(tiles):
                if t == 0:
                    tables(g)
                    if g + 1 < 4:
                        head_load(g + 1)
                inproj(g, t)
                if i >= 2:
                    post_b(*tiles[i - 2])
                if b == 0 and late_casts:
                    cast_piece(*late_casts.pop(0))
                if g + 1 < 4:
                    if t < 2:
                        head_tile(2 * t)
                        head_tile(2 * t + 1)
                    else:
                        hd, ht, hx = hbufs[(g + 1) % 2]
                        for kc in range(4 * (t - 2), 4 * (t - 2) + 4):
                            transpose_kc(kc, g1, T("g1"), hd, ht, extra=hx)
                post(g, t)
                if i >= 1:
                    post_add(*tiles[i - 1])
            post_add(*tiles[15])
            post_b(*tiles[14])
            post_b(*tiles[15])

        ctr = {"pt": 0, "ds": 0, "ws": 0, "z1": 0}

        def chunk(b, c, stage=lambda n: None, prev_tail=(), pre_done=0):
            prev_tail = list(prev_tail)
            j0 = c * 512
            r0 = b * SEQ + c * 512
            steps = [(j, kb) for j in range(4) for kb in range(16)]

            def qk_op(s):
                j, kb = steps[s]
                sg = s % 2

                def f(e):
                    e.matmul(ps[:, (2 * sg) * 512:(2 * sg + 1) * 512], lhsT=kT[0:64, kb * 128:(kb + 1) * 128],
                             rhs=qT[0:64, j, j0:j0 + 512], start=True, stop=True)
                    return e.matmul(ps[:, (2 * sg + 1) * 512:(2 * sg + 2) * 512], lhsT=kT[64:128, kb * 128:(kb + 1) * 128],
                                    rhs=qT[64:128, j, j0:j0 + 512], start=True, stop=True)
                P.op(PE, f, reads=[T("kT"), T("qT")], writes=hb(2 * sg, 2))

            def pv_op(s):
                j, kb = steps[s]
                sg = s % 2
                slot = ctr["pt"] % NPT
                ctr["pt"] += 1
                P.op(ACT, lambda e: e.activation(out=PT[slot], in_=ps[:, 2 * sg * 512:(2 * sg + 2) * 512], func=AF.Exp,
                                                 bias=nbias[:], scale=1.0),
                     reads=hb(2 * sg, 2) + [T("nbias"), G2], writes=[T("PT%d" % slot)])

                def f(e):
                    e.matmul(ps[:, 4 * 512:5 * 512], lhsT=vp[:, kb, 0:128], rhs=PT[slot][:, 0:512],
                             start=(kb == 0), stop=(kb == 15))
                    return e.matmul(ps[:, 5 * 512:6 * 512], lhsT=vp[:, kb, 128:256], rhs=PT[slot][:, 512:1024],
                                    start=(kb == 0), stop=(kb == 15))
                P.op(PE, f, reads=[T("vp"), T("PT%d" % slot), G2], writes=hb(4, 2))
                if kb == 15:
                    P.op(DVE, lambda e: e.tensor_copy(ocp[:], ps[:, 4 * 512:6 * 512]), reads=hb(4, 2), writes=[T("ocp")])
                    for hp in range(2):
                        orow = slice(hp * 64, hp * 64 + 64)
                        srow = slice((1 - hp) * 64, (1 - hp) * 64 + 64)
                        cs = slice(hp * 512, (hp + 1) * 512)
                        if j == 3:
                            P.op(ACT, lambda e, orow=orow, srow=srow, cs=cs: e.activation(out=rcp[orow, :], in_=ocp[srow, cs], func=AF.Ln),
                                 reads=[T("ocp")], writes=[T("rcp")])
                            P.op(ACT, lambda e, orow=orow: e.activation(out=rcp[orow, :], in_=rcp[orow, :], func=AF.Exp, scale=-1.0),
                                 reads=[T("rcp")], writes=[T("rcp")])
                        else:
                            P.op(DVE, lambda e, orow=orow, srow=srow, cs=cs: e.reciprocal(rcp[orow, :], ocp[srow, cs]),
                                 reads=[T("ocp")], writes=[T("rcp")])
                        P.op(DVE, lambda e, orow=orow, cs=cs: e.tensor_tensor(
                            out=mixedT[orow, j, :], in0=ocp[orow, cs], in1=rcp[orow, :], op=ALU.mult),
                            reads=[T("ocp"), T("rcp")], writes=[T("mixedT")])

            funits = [(mi, hf, sl) for mi in range(2) for hf in range(2) for sl in range(4)]

            def fourier_micro(m, cn=c):
                u, a = divmod(m, 4)
                mi, hf, sl = funits[u]
                j0 = cn * 512
                if a == 0:
                    dmat = (dftc_d, dfts_d)[mi]
                    slot = ctr["ds"] % NDS
                    ctr["ds"] += 1
                    ctr["cur_ds"] = slot
                    P.dma(SP, lambda e: e.dma_start(
                        out=dsl[slot], in_=dmat[sl * 512:(sl + 1) * 512, j0:j0 + 512].rearrange("(a p) j -> p a j", p=128)),
                        "ds%d" % slot, reads=[G2], writes=[T("dsl%d" % slot)])
                slot = ctr["cur_ds"]
                kb = sl * 4 + a

                def f(e):
                    for ci in range(2):
                        cc = 2 * hf + ci
                        bk = 6 + ci
                        r = e.matmul(ps[:, bk * 512:(bk + 1) * 512], lhsT=U[:, kb, cc * 128:(cc + 1) * 128],
                                     rhs=dsl[slot][:, a, :], start=(kb == 0), stop=(kb == 15))
                    return r
                P.op(PE, f, reads=[T("U"), T("dsl%d" % slot), G2], writes=hb(6, 2))
                if sl == 3 and a == 3:
                    for ci in range(2):
                        cc = 2 * hf + ci
                        bk = 6 + ci
                        ab = mi * 4 + cc
                        P.op(DVE, lambda e, bk=bk, ab=ab: e.tensor_copy(atbt[:, ab, :], ps[:, bk * 512:(bk + 1) * 512]),
                             reads=hb(bk, 1), writes=[T("atbt%d" % ab)])

            qk_op(0)
            qk_op(1)
            nfu = pre_done
            nmic = 4 * len(funits)
            pace = 1 if pre_done == 0 else 2
            for s in range(len(steps)):
                pv_op(s)
                if s + 2 < len(steps):
                    qk_op(s + 2)
                if s >= (6 if pace == 1 else 0) and (s % pace == 0) and nfu < nmic:
                    fourier_micro(nfu)
                    nfu += 1
                if s >= 2 and prev_tail:
                    prev_tail.pop(0)()
            while nfu < nmic:
                fourier_micro(nfu)
                nfu += 1
            while prev_tail:
                prev_tail.pop(0)()
            stage("attn")
            for cc in range(4):
                def f(e, cc=cc):
                    e.matmul(ps[:, cc * 512:(cc + 1) * 512], lhsT=bdm[:, cc, :], rhs=atbt[:, cc, :], start=True, stop=False)
                    return e.matmul(ps[:, cc * 512:(cc + 1) * 512], lhsT=bdm[:, 4 + cc, :], rhs=atbt[:, 4 + cc, :], start=False, stop=True)
                P.op(PE, f, reads=[T("bdm"), T("atbt%d" % cc), T("atbt%d" % (4 + cc))], writes=hb(cc, 1))
                if cc % 2 == 0:
                    P.op(DVE, lambda e, cc=cc: e.tensor_copy(mixedT[:, 4 + cc, :], ps[:, cc * 512:(cc + 1) * 512]),
                         reads=hb(cc, 1), writes=[T("mixedT")])
                else:
                    P.op(ACT, lambda e, cc=cc: e.activation(out=mixedT[:, 4 + cc, :], in_=ps[:, cc * 512:(cc + 1) * 512], func=AF.Copy),
                         reads=hb(cc, 1), writes=[T("mixedT")])

            stage("fourier")
            prefetch = (c + 1 < 4) and stop is None
            npre = [0]
            P.dma(SP, lambda e: e.dma_start(out=xbuf[:], in_=x_d[r0:r0 + 512, :].rearrange("(t p) d -> p t d", p=128)),
                  "xl", writes=[T("xb%d" % t) for t in range(4)])
            for t in range(4):
                pb = (4, 0, 2, 4)[t]

                def f(e, t=t, pb=pb):
                    for m in range(8):
                        for h in range(2):
                            r = e.matmul(ps[:, (pb + h) * 512:(pb + h + 1) * 512], lhsT=mixedT[:, m, t * 128:(t + 1) * 128],
                                         rhs=wout[:, m, h * 512:(h + 1) * 512], start=(m == 0), stop=(m == 7))
                    return r
                P.op(PE, f, reads=[T("mixedT"), T("wout")], writes=hb(pb, 2))
                P.op(DVE, lambda e, t=t, pb=pb: e.tensor_tensor(out=xbuf[:, t, :], in0=ps[:, pb * 512:(pb + 2) * 512],
                                                                 in1=xbuf[:, t, :], op=ALU.add),
                     reads=hb(pb, 2) + [T("xb%d" % t)], writes=[T("xb%d" % t)])
                rms_stats(xbuf[:, t, :], t, [T("xb%d" % t)], "c")
                P.op(DVE, lambda e, t=t: e.tensor_scalar(xn[:, t, :], xbuf[:, t, :], rstd[:, t:t + 1], None, op0=ALU.mult),
                     reads=[T("xb%d" % t), T("rstd%d" % t)], writes=[T("xn%d" % t)])
                if prefetch:
                    for _ in range(6):
                        fourier_micro(npre[0], c + 1)
                        npre[0] += 1
            for kc in range(8):
                transpose_kc(kc, g2, T("g2"), hT, "hT")
                if prefetch:
                    for _ in range(3):
                        fourier_micro(npre[0], c + 1)
                        npre[0] += 1

            stage("outproj")
            for fs in range(8):
                slot = ctr["ws"] % NWS
                ctr["ws"] += 1
                wv = wsl[slot][:, :].rearrange("p (k f) -> p k f", f=512)
                P.dma(SP, lambda e, fs=fs, wv=wv: e.dma_start(
                    out=wv, in_=wup_bf_d[:, fs * 512:(fs + 1) * 512].rearrange("(k p) f -> p k f", p=128)),
                    "ws%d" % slot, reads=[T("wup_bf%d" % i) for i in range(8)], writes=[T("wsl%d" % slot)])
                for fi in range(4):
                    fc = fs * 4 + fi
                    bk = fc % 8

                    def f(e, wv=wv, fi=fi, bk=bk):
                        for kc in range(8):
                            r = e.matmul(ps[:, bk * 512:(bk + 1) * 512], lhsT=wv[:, kc, fi * 128:(fi + 1) * 128],
                                         rhs=hT[:, kc, :], start=(kc == 0), stop=(kc == 7))
                        return r
                    P.op(PE, f, reads=[T("wsl%d" % slot)] + [T("hT%d" % kc) for kc in range(8)], writes=hb(bk, 1))
                    zs = ctr["z1"] % 2
                    ctr["z1"] += 1
                    P.op(ACT, lambda e, bk=bk, zs=zs: e.activation(out=z1[zs][:], in_=ps[:, bk * 512:(bk + 1) * 512], func=AF.Relu),
                         reads=hb(bk, 1), writes=[T("z1_%d" % zs)])
                    P.op(POOL, lambda e, fc=fc, zs=zs: e.tensor_tensor(out=zT[:, fc, :], in0=z1[zs][:], in1=z1[zs][:], op=ALU.mult),
                         reads=[T("z1_%d" % zs), G1], writes=[T("zT")])

            stage("mlpup")
            for dsb in range(8):
                slot = ctr["ws"] % NWS
                ctr["ws"] += 1
                wv = wsl[slot][:, :].rearrange("p (a n) -> p a n", n=1024)
                P.dma(SP, lambda e, dsb=dsb, wv=wv: e.dma_start(
                    out=wv, in_=wdown_bf_d[dsb * 512:(dsb + 1) * 512, :].rearrange("(a p) n -> p a n", p=128)),
                    "ws%d" % slot, reads=[T("wdown_bf%d" % dsb)], writes=[T("wsl%d" % slot)])
                for t in range(4):
                    for h in range(2):
                        bk = t * 2 + h

                        def f(e, wv=wv, dsb=dsb, t=t, h=h, bk=bk):
                            for a in range(4):
                                fc = dsb * 4 + a
                                r = e.matmul(ps[:, bk * 512:(bk + 1) * 512], lhsT=zT[:, fc, t * 128:(t + 1) * 128],
                                             rhs=wv[:, a, h * 512:(h + 1) * 512], start=(fc == 0), stop=(fc == 31))
                            return r
                        P.op(PE, f, reads=[T("wsl%d" % slot), T("zT"), G1], writes=hb(bk, 1))
            for t in range(4):
                P.op(DVE, lambda e, t=t: e.tensor_tensor(out=xbuf[:, t, :], in0=ps[:, 2 * t * 512:(2 * t + 2) * 512],
                                                          in1=xbuf[:, t, :], op=ALU.add),
                     reads=hb(2 * t, 2) + [T("xb%d" % t)], writes=[T("xb%d" % t)])
            tail = []
            for t in range(4):
                def stats(t=t):
                    rms_stats(xbuf[:, t, :], t, [T("xb%d" % t)], "f")

                def scale(t=t):
                    P.op(DVE, lambda e: e.scalar_tensor_tensor(out=xbuf[:, t, :], in0=xbuf[:, t, :], scalar=rstd[:, t:t + 1],
                                                               in1=gf_bc[:], op0=ALU.mult, op1=ALU.mult),
                         reads=[T("xb%d" % t), T("rstd%d" % t), T("gf")], writes=[T("xb%d" % t)])
                tail.append(stats)
                tail.append(scale)

            def store():
                P.dma(SP, lambda e: e.dma_start(out=out_d[r0:r0 + 512, :].rearrange("(t p) d -> p t d", p=128), in_=xbuf[:]),
                      "os", reads=[T("xb%d" % t) for t in range(4)], final=True)
            tail.append(store)
            if stop == "chunk":
                for f_ in tail:
                    f_()
                tail = []
            stage("chunk")
            return tail, npre[0]

        def stage(name):
            if stop == name:
                raise _Stop()

        try:
            stage("setup")
            for b in range(nseq):
                fence()
                phase_a(b)
                while b == 0 and late_casts:
                    cast_piece(*late_casts.pop(0))
                stage("phaseA")
                fence()
                tail, npre_ = [], 0
                for c in range(4):
                    tail, npre_ = chunk(b, c, stage, tail, npre_)
                for f_ in tail:
                    f_()
        except _Stop:
            pass
        if stop is not None:
            for e_ in ENGINES:
                for o_ in P.ops[e_]:
                    if o_.is_dma and o_ not in P.final_waits:
                        P.final_waits.append(o_)
        if dumps:
            allb = list(B.values()) + HB
            loc = dict(locals())
            for nm in dumps:
                ap = loc[nm]
                shp = list(ap.shape)
                dd = nc.dram_tensor("dbg_" + nm, shp, ap.dtype, kind="ExternalOutput").ap()
                src = ap if isinstance(ap, bass.AP) else ap[:]
                P.dma(SP, lambda e, dd=dd, src=src: e.dma_start(out=dd, in_=src), "dbg_" + nm, reads=allb, final=True)
        P.emit(nc, st)
    return nc


def _constants():
    bf = ml_dtypes.bfloat16
    k = np.arange(SEQ, dtype=np.int64)
    kj = (k[:, None] * k[None, :]) % SEQ
    ang = kj.astype(np.float64) * (2.0 * np.pi / SEQ)
    dftc = np.cos(ang).astype(np.float32).astype(bf)
    dfts = np.sin(ang).astype(np.float32).astype(bf)
    c = np.arange(64, dtype=np.int64)
    cang = ((c[:, None] * c[None, :]) % 64).astype(np.float64) * (2.0 * np.pi / 64)
    norm = 1.0 / np.sqrt(SEQ * 64.0)
    cc = np.cos(cang) * norm
    sc = -np.sin(cang) * norm
    z = np.zeros((64, 64))
    ccbd = np.block([[cc, z], [z, cc]]).astype(np.float32).astype(bf)
    scbd = np.block([[sc, z], [z, sc]]).astype(np.float32).astype(bf)
    s = np.arange(SEQ)
    row = (s // 64).astype(np.float32)
    col = (s % 64).astype(np.float32)
    inv = (np.float32(10000.0) ** (-np.arange(0, 32, 2, dtype=np.float32) / np.float32(32))).astype(np.float32)
    ra = row[:, None] * inv[None, :]
    ca = col[:, None] * inv[None, :]
    ropec = np.concatenate([np.cos(ra), np.cos(ra), np.cos(ca), np.cos(ca)], axis=1).astype(np.float32)
    ropes = np.concatenate([-np.sin(ra), np.sin(ra), -np.sin(ca), np.sin(ca)], axis=1).astype(np.float32)
    return dftc, dfts, ccbd, scbd, ropec, ropes


def _swap(g):
    return np.concatenate([g[16:32], g[0:16], g[48:64], g[32:48]])


_CACHE = {}


def kernel(x, mix_norm_g, w_in, q_norm_g, k_norm_g, w_fourier, w_out,
           mlp_norm_g, w_up, w_down, final_norm_g):
    f32 = np.float32
    x = np.asarray(x, f32)
    w_in = np.asarray(w_in, f32)
    w_out = np.asarray(w_out, f32)
    if "nc" not in _CACHE:
        _CACHE["nc"] = build_nc()
        _CACHE["const"] = _constants()
    nc = _CACHE["nc"]
    dftc, dfts, ccbd, scbd, ropec, ropes = _CACHE["const"]
    qcols = np.concatenate([np.concatenate([np.arange(j * 64, (j + 1) * 64), np.arange((j + 4) * 64, (j + 5) * 64)])
                            for j in range(4)])
    cols = np.concatenate([qcols, np.arange(512, 1280)])
    w_in_p = np.ascontiguousarray(w_in[:, cols])
    rows = np.concatenate([qcols, np.arange(512, 1024)])
    w_out_p = np.ascontiguousarray(w_out[rows, :])
    gq = np.asarray(q_norm_g, f32)
    gk = np.asarray(k_norm_g, f32)
    g_qk = np.concatenate([gq, _swap(gq), gk, _swap(gk)]).reshape(1, 256).astype(f32)
    common = {
        "w_in": w_in_p, "w_out": w_out_p,
        "w_up": np.ascontiguousarray(np.asarray(w_up, f32)),
        "w_down": np.ascontiguousarray(np.asarray(w_down, f32)),
        "wf": np.ascontiguousarray(np.asarray(w_fourier, f32).reshape(512, 64)),
        "g_mix": np.asarray(mix_norm_g, f32).reshape(1, DM),
        "g_mlp": np.asarray(mlp_norm_g, f32).reshape(1, DM),
        "g_fin": np.asarray(final_norm_g, f32).reshape(1, DM),
        "g_qk": g_qk,
        "dftc": dftc, "dfts": dfts, "ropec": ropec, "ropes": ropes, "ccbd": ccbd, "scbd": scbd,
    }
    in_maps = []
    for c in range(N_CORES):
        m = dict(common)
        m["x"] = np.ascontiguousarray(x[c * NSEQ:(c + 1) * NSEQ].reshape(NSEQ * SEQ, DM))
        in_maps.append(m)
    res = run_bass_kernel_spmd(nc, in_maps, core_ids=list(range(N_CORES)))
    outs = [np.asarray(r["out"], f32).reshape(NSEQ, SEQ, DM) for r in res.results]
    return np.concatenate(outs, axis=0)
```

```python
from contextlib import ExitStack
import numpy as np
import ml_dtypes
import concourse.bass as bass
import concourse.mybir as mybir
from concourse.bass_utils import run_bass_kernel_spmd

F32 = mybir.dt.float32
BF16 = mybir.dt.bfloat16
ALU = mybir.AluOpType
AF = mybir.ActivationFunctionType
AX = mybir.AxisListType

PE, ACT, DVE, POOL, SP = "tensor", "scalar", "vector", "gpsimd", "sync"
ENGINES = (PE, ACT, DVE, POOL, SP)

N_CORES = 8
SEQ = 2048
DM = 1024
DFF = 4096
NSEQ = 2
EPS = 1e-6


class Buf:
    __slots__ = ("name", "writer", "readers")

    def __init__(self, name):
        self.name = name
        self.writer = None
        self.readers = []


class Op:
    __slots__ = ("eng", "fn", "deps", "is_dma", "key", "signal", "tok")

    def __init__(self, eng, fn, is_dma=False, key=None):
        self.eng = eng
        self.fn = fn
        self.deps = {}
        self.is_dma = is_dma
        self.key = key
        self.signal = False
        self.tok = None


class Prog:
    def __init__(self):
        self.ops = {e: [] for e in ENGINES}
        self.final_waits = []

    def _track(self, op, reads, writes):
        deps = []
        for b in reads:
            if b.writer is not None:
                deps.append((b.writer, "RAW"))
        for b in writes:
            if b.writer is not None:
                deps.append((b.writer, "WAW"))
            for r in b.readers:
                deps.append((r, "WAR"))
        for d, kind in deps:
            if d is op:
                continue
            if d.eng == op.eng and not d.is_dma and not op.is_dma and op.eng != POOL:
                if kind == "WAW" or op.eng == PE:
                    continue
            if id(d) not in op.deps:
                op.deps[id(d)] = d
                d.signal = True
        for b in reads:
            b.readers.append(op)
        for b in writes:
            b.writer = op
            b.readers = []

    def op(self, eng, fn, reads=(), writes=()):
        o = Op(eng, fn)
        self._track(o, reads, writes)
        self.ops[eng].append(o)
        return o

    def dma(self, eng, fn, key, reads=(), writes=(), final=False):
        o = Op(eng, fn, is_dma=True, key=key)
        o.signal = True
        self._track(o, reads, writes)
        self.ops[eng].append(o)
        if final:
            self.final_waits.append(o)
        return o

    def emit(self, nc, stack):
        keycount = {}
        for e in ENGINES:
            for i, o in enumerate(self.ops[e]):
                if o.is_dma:
                    keycount[o.key] = keycount.get(o.key, 0) + 16
                    o.tok = ("d_" + o.key, keycount[o.key])
                else:
                    o.tok = ("e_" + e, i + 1)

        def plan(e):
            waited = {}
            out = []
            for o in self.ops[e]:
                need = {}
                for d in o.deps.values():
                    sn, v = d.tok
                    if need.get(sn, 0) < v:
                        need[sn] = v
                ws = []
                for sn, v in need.items():
                    if waited.get(sn, 0) >= v:
                        continue
                    ws.append((sn, v))
                    waited[sn] = v
                out.append(ws)
            fin = {}
            if e == SP:
                for d in self.final_waits:
                    sn, v = d.tok
                    if fin.get(sn, 0) < v:
                        fin[sn] = v
            return out, list(fin.items())

        plans = {e: plan(e) for e in ENGINES}
        used = {}
        for e in ENGINES:
            ws_list, fin = plans[e]
            for ws in ws_list:
                for sn, v in ws:
                    used.setdefault(sn, set()).add(v)
            for sn, v in fin:
                used.setdefault(sn, set()).add(v)
        final_val = {}
        for e in ENGINES:
            sn = "e_" + e
            vals = sorted(used.get(sn, ()))
            final_val[sn] = {v: r + 1 for r, v in enumerate(vals)}
        sems = {}
        for sn in used:
            sems[sn] = stack.enter_context(nc.semaphore("s_" + sn))
        for e in ENGINES:
            for o in self.ops[e]:
                if o.is_dma and o.tok[0] not in sems:
                    sems[o.tok[0]] = stack.enter_context(nc.semaphore("s_" + o.tok[0]))
        block = stack.enter_context(nc.Block())
        prog = self

        def xl(sn, v):
            return final_val[sn][v] if sn in final_val and sn.startswith("e_") else v

        def make(e):
            def body(eng):
                ws_list, fin = plans[e]
                for o, ws in zip(prog.ops[e], ws_list):
                    for sn, v in ws:
                        eng.wait_ge(sems[sn], xl(sn, v))
                    ins = o.fn(eng)
                    sn, v = o.tok
                    if o.is_dma:
                        ins.then_inc(sems[sn], 16)
                    elif v in final_val.get(sn, ()):
                        ins.then_inc(sems[sn], 1)
                for sn, v in fin:
                    eng.wait_ge(sems[sn], xl(sn, v))
            return body

        block.tensor(make(PE))
        block.scalar(make(ACT))
        block.vector(make(DVE))
        block.gpsimd(make(POOL))
        block.sync(make(SP))
        self.nsems = len(sems)
        self.nsignals = {sn: len(m) for sn, m in final_val.items()}


class _Stop(Exception):
    pass


_DBG = {"nolate": False}


def build_nc(nseq=NSEQ, stop=None, dumps=()):
    nc = bass.Bass("TRN2", target_bir_lowering=False)
    NTOK = nseq * SEQ

    def din(name, shape, dt=F32):
        return nc.dram_tensor(name, shape, dt, kind="ExternalInput").ap()

    x_d = din("x", [NTOK, DM])
    w_in_d = din("w_in", [DM, 1280])
    w_out_d = din("w_out", [DM, DM])
    w_up_d = din("w_up", [DM, DFF])
    w_down_d = din("w_down", [DFF, DM])
    wf_d = din("wf", [512, 64])
    gmix_d = din("g_mix", [1, DM])
    gmlp_d = din("g_mlp", [1, DM])
    gfin_d = din("g_fin", [1, DM])
    gqk_d = din("g_qk", [1, 256])
    dftc_d = din("dftc", [SEQ, SEQ], BF16)
    dfts_d = din("dfts", [SEQ, SEQ], BF16)
    ropec_d = din("ropec", [SEQ, 64])
    ropes_d = din("ropes", [SEQ, 64])
    ccbd_d = din("ccbd", [128, 128], BF16)
    scbd_d = din("scbd", [128, 128], BF16)
    out_d = nc.dram_tensor("out", [NTOK, DM], F32, kind="ExternalOutput").ap()
    win_bf_d = nc.dram_tensor("win_bf", [DM, 1280], BF16, kind="Internal").ap()
    wout_bf_d = nc.dram_tensor("wout_bf", [DM, DM], BF16, kind="Internal").ap()
    wup_bf_d = nc.dram_tensor("wup_bf", [DM, DFF], BF16, kind="Internal").ap()
    wdown_bf_d = nc.dram_tensor("wdown_bf", [DFF, DM], BF16, kind="Internal").ap()

    P = Prog()
    st = ExitStack()
    with st:
        def sb(name, shape, dt):
            return st.enter_context(nc.sbuf_tensor(name, shape, dt))

        ident = sb("ident", [128, 128], BF16)
        g1 = sb("g1", [128, 8], F32)
        g2 = sb("g2", [128, 8], F32)
        gf_bc = sb("gf_bc", [128, DM], F32)
        gqk = sb("gqk", [128, 4, 64], F32)
        gmax = sb("gmax", [128, 4], F32)
        nbias = sb("nbias", [128, 1], F32)
        epst = sb("epst", [128, 1], F32)
        ccbd = sb("ccbd_sb", [128, 128], BF16)
        scbd = sb("scbd_sb", [128, 128], BF16)
        bdm = sb("bdm", [128, 8, 128], BF16)
        wout = sb("wout_sb", [128, 8, DM], BF16)
        vp = sb("vp", [128, 16, 256], BF16)
        qT = sb("qT", [128, 4, SEQ], BF16)
        kT = sb("kT", [128, SEQ], BF16)
        ocp = sb("ocp", [128, 1024], F32)
        U = sb("U", [128, 16, 512], BF16)
        r1 = sb("r1", [128, 32 * 512], BF16)
        win = r1[:, 0:8 * 1280].rearrange("p (k n) -> p k n", k=8)
        zT = r1[:, :].rearrange("p (f t) -> p f t", t=512)
        hT_b = r1[:, 10240:14336].rearrange("p (k t) -> p k t", t=512)
        xbuf = sb("xbuf", [128, 4, DM], F32)
        xn = sb("xn", [128, 4, DM], BF16)
        hT = sb("hT", [128, 8, 512], BF16)
        junk = sb("junk", [128, DM], BF16)
        fz = sb("fz", [128, 8], BF16)
        ss = sb("ss", [128, 4], F32)
        lnv = sb("lnv", [128, 4], F32)
        rstd = sb("rstd", [128, 4], F32)
        r2 = sb("r2", [128, 12288], BF16)
        r2f = r2[:, :].bitcast(F32)
        rC = r2f[:, 0:256].rearrange("p (t d) -> p t d", d=64)
        rS = r2f[:, 256:512].rearrange("p (t d) -> p t d", d=64)
        Cq = r2f[:, 512:768].rearrange("p (t d) -> p t d", d=64)
        Sq = r2f[:, 768:1024].rearrange("p (t d) -> p t d", d=64)
        Ck = r2f[:, 1024:1280].rearrange("p (t d) -> p t d", d=64)
        Sk = r2f[:, 1280:1536].rearrange("p (t d) -> p t d", d=64)
        SETS = []
        for si in range(2):
            base = 1536 + si * 2240
            SETS.append(dict(
                nq=r2f[:, base:base + 640],
                t1=r2f[:, base + 640:base + 1280],
                t2=r2f[:, base + 1280:base + 1920],
                qk=r2[:, 2 * (base + 1920):2 * (base + 1920) + 640],
                ssq=r2f[:, 6016 + si * 48:6016 + si * 48 + 10],
                lnq=r2f[:, 6016 + si * 48 + 16:6016 + si * 48 + 26],
                rs10=r2f[:, 6016 + si * 48 + 32:6016 + si * 48 + 42],
            ))
        NPT = 4
        PT = [r2[:, i * 1024:(i + 1) * 1024] for i in range(NPT)]
        NDS = 4
        dsl = [r2[:, 4096 + i * 2048: 4096 + (i + 1) * 2048].rearrange("p (a j) -> p a j", j=512)
               for i in range(NDS)]
        mixedT = sb("mixedT", [128, 8, 512], BF16)
        atbt = sb("atbt", [128, 8, 512], BF16)
        wfz = atbt[:, 0:2, :].rearrange("p a t -> p (a t)").bitcast(F32).rearrange("p (i d) -> p i d", d=128)
        wfb = atbt[:, 2, :].rearrange("p (i d) -> p i d", d=128)
        gtmp = atbt[:, 3, :].bitcast(F32).rearrange("p (i d) -> p i d", d=64)
        identz = atbt[:, 4, 0:128]
        NWS = 3
        wsl = [sb("wsl%d" % i, [128, 4096], BF16) for i in range(NWS)]
        rcp = sb("rcp", [128, 512], F32)
        z1 = [sb("z1_%d" % i, [128, 512], BF16) for i in range(2)]

        ps = st.enter_context(nc.psum_tensor("ps", [128, 4096], F32))
        psb = ps[:, :].bitcast(BF16)
        HB = [Buf("bank%d" % i) for i in range(8)]

        def hb(i, n=1):
            return HB[i:i + n]

        B = {}

        def T(name):
            if name not in B:
                B[name] = Buf(name)
            return B[name]

        G1 = T("guard_r1")
        G2 = T("guard_r2")

        P.op(POOL, lambda e: e.memset(identz, 0.0), writes=[T("atbt4")])
        P.op(POOL, lambda e: e.affine_select(out=ident[:], in_=identz, compare_op=ALU.not_equal, fill=1.0,
                                             base=0, pattern=[[-1, 128]], channel_multiplier=1),
             reads=[T("atbt4")], writes=[T("ident")])
        P.op(POOL, lambda e: e.memset(vp[:, :, 64:192], 1.0), writes=[T("vp_ones")])
        P.op(POOL, lambda e: e.memset(epst[:], EPS), writes=[T("epst")])
        P.op(POOL, lambda e: e.memset(ocp[:], 0.0), writes=[T("ocp")])
        P.op(POOL, lambda e: e.memset(wfz, 0.0), writes=[T("atbt0"), T("atbt1")])

        def cast_piece(dst, src, r0, r1_, key, tokname):
            P.dma(POOL, lambda e: e.dma_start(out=dst[r0:r1_, :], in_=src[r0:r1_, :]), key, writes=[T(tokname)])
        for i in range(2):
            cast_piece(win_bf_d, w_in_d, i * 512, (i + 1) * 512, "c_win%d" % i, "win_bf%d" % i)
        for i in range(2):
            cast_piece(wout_bf_d, w_out_d, i * 512, (i + 1) * 512, "c_wout%d" % i, "wout_bf%d" % i)
        late_casts = []
        wup_bf_v = wup_bf_d.rearrange("r (a n) -> (r a) n", n=1024)
        w_up_v = w_up_d.rearrange("r (a n) -> (r a) n", n=1024)
        for i in range(8):
            late_casts.append((wup_bf_v, w_up_v, i * 512, (i + 1) * 512, "c_wup%d" % i, "wup_bf%d" % i))
        for i in range(8):
            late_casts.append((wdown_bf_d, w_down_d, i * 512, (i + 1) * 512, "c_wdn%d" % i, "wdown_bf%d" % i))
        if _DBG["nolate"]:
            late_casts = []

        P.dma(SP, lambda e: e.dma_start(out=g1[:], in_=gmix_d.rearrange("o (k p) -> p (o k)", p=128),
                                        allow_slow_non_contiguous=True), "k_g1", writes=[T("g1")])
        P.dma(SP, lambda e: e.dma_start(out=gqk[:].rearrange("p a d -> p (a d)"), in_=gqk_d.partition_broadcast(128)),
              "k_gqk", writes=[T("gqk")])
        P.dma(SP, lambda e: e.dma_start(out=ccbd[:], in_=ccbd_d), "k_ccbd", writes=[T("ccbd")])
        P.dma(SP, lambda e: e.dma_start(out=scbd[:], in_=scbd_d), "k_scbd", writes=[T("scbd")])
        for i in range(4):
            P.dma(SP, lambda e, i=i: e.dma_start(out=wfz[0:64, i, 0:64], in_=wf_d[(2 * i) * 64:(2 * i + 1) * 64, :]),
                  "k_wf%d" % (2 * i), reads=[T("atbt0"), T("atbt1")], writes=[T("wfz_a%d" % i)])
            P.dma(SP, lambda e, i=i: e.dma_start(out=wfz[64:128, i, 64:128], in_=wf_d[(2 * i + 1) * 64:(2 * i + 2) * 64, :]),
                  "k_wf%d" % (2 * i + 1), reads=[T("atbt0"), T("atbt1")], writes=[T("wfz_b%d" % i)])
        P.dma(SP, lambda e: e.dma_start(out=g2[:], in_=gmlp_d.rearrange("o (k p) -> p (o k)", p=128),
                                        allow_slow_non_contiguous=True), "k_g2", writes=[T("g2")])
        P.dma(SP, lambda e: e.dma_start(out=gf_bc[:], in_=gfin_d.partition_broadcast(128)), "k_gf", writes=[T("gf")])

        P.op(DVE, lambda e: e.tensor_scalar(gtmp, gqk[:], -1.0, None, op0=ALU.mult), reads=[T("gqk")], writes=[T("atbt3")])
        P.op(DVE, lambda e: e.tensor_tensor(out=gtmp, in0=gtmp, in1=gqk[:], op=ALU.max), reads=[T("atbt3"), T("gqk")], writes=[T("atbt3")])
        P.op(DVE, lambda e: e.tensor_reduce(out=gmax[:], in_=gtmp, axis=AX.X, op=ALU.max), reads=[T("atbt3")], writes=[T("gmax")])
        P.op(DVE, lambda e: e.tensor_tensor(out=nbias[:], in0=gmax[:, 0:1], in1=gmax[:, 2:3], op=ALU.mult), reads=[T("gmax")], writes=[T("nbias")])
        P.op(DVE, lambda e: e.tensor_scalar(nbias[:], nbias[:], -8.0, None, op0=ALU.mult), reads=[T("nbias")], writes=[T("nbias")])

        gqs = sb("gqs", [128, 4, 64], F32)

        def mk_gqs(e):
            e.tensor_scalar(gqs[:, 0:2, :], gqk[:, 0:2, :], 0.125, None, op0=ALU.mult)
            return e.tensor_copy(gqs[:, 2:4, :], gqk[:, 2:4, :])
        P.op(DVE, mk_gqs, reads=[T("gqk")], writes=[T("gqs")])

        P.op(DVE, lambda e: e.tensor_copy(wfb, wfz),
             reads=[T("atbt0"), T("atbt1")] + [T("wfz_a%d" % i) for i in range(4)] + [T("wfz_b%d" % i) for i in range(4)], writes=[T("atbt2")])

        def bd_mm(e):
            for i in range(4):
                e.matmul(ps[:, i * 128:(i + 1) * 128], lhsT=ccbd[:, :], rhs=wfb[:, i, :], start=True, stop=True)
            for i in range(4):
                r = e.matmul(ps[:, 512 + i * 128:512 + (i + 1) * 128], lhsT=scbd[:, :], rhs=wfb[:, i, :], start=True, stop=True)
            return r
        P.op(PE, bd_mm, reads=[T("ccbd"), T("scbd"), T("atbt2")], writes=hb(0, 2))
        P.op(DVE, lambda e: e.tensor_copy(bdm[:].rearrange("p a d -> p (a d)"), ps[:, 0:1024]), reads=hb(0, 2), writes=[T("bdm")])

        P.dma(SP, lambda e: e.dma_start(out=wout[:], in_=wout_bf_d.rearrange("(k p) n -> p k n", p=128)), "k_wout",
              reads=[T("wout_bf0"), T("wout_bf1")], writes=[T("wout")])

        def fence():
            P.op(POOL, lambda e: e.memset(fz[:], 0.0), writes=[G1, G2, T("fz")])

        def rms_stats(src_ap, col, src_toks, tag):
            P.op(ACT, lambda e: e.activation(out=junk[:], in_=src_ap, func=AF.Square, accum_out=ss[:, col:col + 1]),
                 reads=src_toks, writes=[T("junk"), T("ss%d" % col)])
            P.op(ACT, lambda e: e.activation(out=lnv[:, col:col + 1], in_=ss[:, col:col + 1], func=AF.Ln,
                                             bias=epst[:], scale=1.0 / DM),
                 reads=[T("ss%d" % col), T("epst")], writes=[T("lnv%d" % col)])
            P.op(ACT, lambda e: e.activation(out=rstd[:, col:col + 1], in_=lnv[:, col:col + 1], func=AF.Exp, scale=-0.5),
                 reads=[T("lnv%d" % col)], writes=[T("rstd%d" % col)])

        def transpose_kc(kc, gvec, gtok, hdst, htag, extra=()):
            h = kc % 2

            def tr(e):
                for t in range(4):
                    r = e.transpose(psb[:, h * 1024 + t * 128: h * 1024 + (t + 1) * 128],
                                    xn[:, t, kc * 128:(kc + 1) * 128], ident[:])
                return r
            P.op(PE, tr, reads=[T("xn%d" % t) for t in range(4)] + [T("ident")], writes=[HB[h]])
            P.op(DVE, lambda e: e.tensor_scalar(hdst[:, kc, :], psb[:, h * 1024:h * 1024 + 512],
                                                gvec[:, kc:kc + 1], None, op0=ALU.mult),
                 reads=[HB[h], gtok] + list(extra), writes=[T("%s%d" % (htag, kc))])

        def transposes_to_hT(gvec, gtok):
            for kc in range(8):
                transpose_kc(kc, gvec, gtok, hT, "hT")

        def phase_a(b):
            P.dma(SP, lambda e: e.dma_start(out=win, in_=win_bf_d.rearrange("(k p) n -> p k n", p=128)), "k_win",
                  reads=[T("win_bf0"), T("win_bf1"), G1], writes=[T("win")])
            hbufs = [(hT, "hT", ()), (hT_b, "hTb", (G1,))]

            def head_load(g):
                r0 = b * SEQ + g * 512
                P.dma(SP, lambda e: e.dma_start(out=xbuf[:], in_=x_d[r0:r0 + 512, :].rearrange("(t p) d -> p t d", p=128)),
                      "xl", writes=[T("xb%d" % t) for t in range(4)])

            def head_tile(t):
                rms_stats(xbuf[:, t, :], t, [T("xb%d" % t)], "a")
                P.op(DVE, lambda e: e.tensor_scalar(xn[:, t, :], xbuf[:, t, :], rstd[:, t:t + 1], None, op0=ALU.mult),
                     reads=[T("xb%d" % t), T("rstd%d" % t)], writes=[T("xn%d" % t)])

            def tables(g):
                s0 = g * 512
                P.dma(SP, lambda e: e.dma_start(out=rC, in_=ropec_d[s0:s0 + 512, :].rearrange("(t p) d -> p t d", p=128)),
                      "rc", reads=[G2], writes=[T("rC")])
                P.dma(SP, lambda e: e.dma_start(out=rS, in_=ropes_d[s0:s0 + 512, :].rearrange("(t p) d -> p t d", p=128)),
                      "rs", reads=[G2], writes=[T("rS")])

                def bc(i):
                    return gqs[:, i, :].unsqueeze(1).to_broadcast([128, 4, 64])
                P.op(POOL, lambda e: e.tensor_tensor(out=Cq, in0=rC, in1=bc(0), op=ALU.mult),
                     reads=[T("rC"), T("gqs"), G2], writes=[T("Cq")])
                P.op(POOL, lambda e: e.tensor_tensor(out=Sq, in0=rS, in1=bc(1), op=ALU.mult),
                     reads=[T("rS"), T("gqs"), G2], writes=[T("Sq")])
                P.op(POOL, lambda e: e.tensor_tensor(out=Ck, in0=rC, in1=bc(2), op=ALU.mult),
                     reads=[T("rC"), T("gqs"), G2], writes=[T("Ck")])
                P.op(POOL, lambda e: e.tensor_tensor(out=Sk, in0=rS, in1=bc(3), op=ALU.mult),
                     reads=[T("rS"), T("gqs"), G2], writes=[T("Sk")])

            def inproj(g, t):
                hsrc, htag, hextra = hbufs[g % 2]
                pb = 2 + 3 * (t % 2)

                def f(e):
                    for kc in range(8):
                        for (c0, c1) in ((0, 512), (512, 1024), (1024, 1280)):
                            r = e.matmul(ps[:, pb * 512 + c0: pb * 512 + c1], lhsT=hsrc[:, kc, t * 128:(t + 1) * 128],
                                         rhs=win[:, kc, c0:c1], start=(kc == 0), stop=(kc == 7))
                    return r
                P.op(PE, f, reads=[T("%s%d" % (htag, kc)) for kc in range(8)] + [T("win"), G1], writes=hb(pb, 3))

            def post(g, t):
                TT = g * 4 + t
                pb = 2 + 3 * (t % 2)
                pin = ps[:, pb * 512: pb * 512 + 1280]
                S_ = SETS[TT % 2]
                sx = "%d" % (TT % 2)
                nq, t1, t2, qk_tm, ssq, lnq, rs10 = S_["nq"], S_["t1"], S_["t2"], S_["qk"], S_["ssq"], S_["lnq"], S_["rs10"]
                sq = t1
                P.op(ACT, lambda e: e.activation(out=sq, in_=pin[:, 0:640], func=AF.Square),
                     reads=hb(pb, 2) + [G2], writes=[T("t1" + sx)])
                P.op(DVE, lambda e: e.tensor_reduce(out=ssq, in_=sq.rearrange("p (h d) -> p h d", d=64), axis=AX.X, op=ALU.add),
                     reads=[T("t1" + sx), G2], writes=[T("ssq" + sx)])
                P.op(ACT, lambda e: e.activation(out=lnq, in_=ssq, func=AF.Ln, bias=epst[:], scale=1.0 / 64),
                     reads=[T("ssq" + sx), T("epst"), G2], writes=[T("lnq" + sx)])
                P.op(ACT, lambda e: e.activation(out=rs10, in_=lnq, func=AF.Exp, scale=-0.5),
                     reads=[T("lnq" + sx), G2], writes=[T("rs10" + sx)])
                P.op(DVE, lambda e: e.tensor_tensor(out=nq.rearrange("p (h d) -> p h d", d=64),
                                                    in0=pin[:, 0:640].rearrange("p (h d) -> p h d", d=64),
                                                    in1=rs10.unsqueeze(2).to_broadcast([128, 10, 64]), op=ALU.mult),
                     reads=hb(pb, 2) + [T("rs10" + sx), G2], writes=[T("nq" + sx)])

                P.op(ACT, lambda e: e.activation(
                    out=vp[:, TT, :].rearrange("p (a d) -> p a d", d=64)[:, 0:4:3, :],
                    in_=pin[:, 640:768].rearrange("p (a d) -> p a d", d=64), func=AF.Copy),
                    reads=hb(pb, 2) + [T("vp_ones")], writes=[T("vp")])
                P.op(ACT, lambda e: e.activation(out=U[:, TT, :], in_=pin[:, 768:1280], func=AF.Copy),
                     reads=hb(pb + 1, 2), writes=[T("U")])

                def mul_c(e):
                    e.tensor_tensor(out=t1[:, 0:512].rearrange("p (h d) -> p h d", d=64),
                                    in0=nq[:, 0:512].rearrange("p (h d) -> p h d", d=64),
                                    in1=Cq[:, t, :].unsqueeze(1).to_broadcast([128, 8, 64]), op=ALU.mult)
                    return e.tensor_tensor(out=t1[:, 512:640].rearrange("p (h d) -> p h d", d=64),
                                           in0=nq[:, 512:640].rearrange("p (h d) -> p h d", d=64),
                                           in1=Ck[:, t, :].unsqueeze(1).to_broadcast([128, 2, 64]), op=ALU.mult)
                P.op(DVE, mul_c, reads=[T("nq" + sx), T("Cq"), T("Ck"), T("ssq" + sx), G2], writes=[T("t1" + sx)])

                def mul_s(e):
                    r = None
                    for (c0, c1, nh, tab) in ((0, 512, 8, Sq), (512, 640, 2, Sk)):
                        av = nq[:, c0:c1].rearrange("p (h x f d) -> p h x f d", x=2, f=2, d=16)
                        ov = t2[:, c0:c1].rearrange("p (h x f d) -> p h x f d", x=2, f=2, d=16)
                        tv = tab[:, t, :].rearrange("p (x f d) -> p x f d", x=2, f=2)
                        e.tensor_tensor(out=ov[:, :, :, 0, :], in0=av[:, :, :, 1, :],
                                        in1=tv[:, :, 0, :].unsqueeze(1).to_broadcast([128, nh, 2, 16]), op=ALU.mult)
                        r = e.tensor_tensor(out=ov[:, :, :, 1, :], in0=av[:, :, :, 0, :],
                                            in1=tv[:, :, 1, :].unsqueeze(1).to_broadcast([128, nh, 2, 16]), op=ALU.mult)
                    return r
                P.op(POOL, mul_s, reads=[T("nq" + sx), T("Sq"), T("Sk"), G2], writes=[T("t2" + sx)])

            def post_add(g, t):
                TT = g * 4 + t
                S_ = SETS[TT % 2]
                sx = "%d" % (TT % 2)
                t1, t2, qk_tm = S_["t1"], S_["t2"], S_["qk"]
                P.op(DVE, lambda e: e.tensor_tensor(out=qk_tm, in0=t1, in1=t2, op=ALU.add),
                     reads=[T("t1" + sx), T("t2" + sx), G2], writes=[T("qk" + sx)])


            def post_b(g, t):
                TT = g * 4 + t
                sx = "%d" % (TT % 2)
                qk_tm = SETS[TT % 2]["qk"]

                def trqk(e):
                    for i in range(5):
                        r = e.transpose(psb[:, 1024 + i * 128: 1024 + (i + 1) * 128], qk_tm[:, i * 128:(i + 1) * 128], ident[:])
                    return r
                P.op(PE, trqk, reads=[T("qk" + sx), T("ident"), G2], writes=hb(1, 1))
                P.op(ACT, lambda e: e.activation(out=qT[:, :, TT * 128:(TT + 1) * 128],
                                                 in_=psb[:, 1024:1536].rearrange("p (j s) -> p j s", s=128), func=AF.Copy),
                     reads=hb(1, 1), writes=[T("qT")])

                def kcopy(e):
                    e.activation(out=kT[0:64, TT * 128:(TT + 1) * 128], in_=psb[0:64, 1536:1664], func=AF.Copy)
                    return e.activation(out=kT[64:128, TT * 128:(TT + 1) * 128], in_=psb[64:128, 1536:1664], func=AF.Copy)
                P.op(ACT, kcopy, reads=hb(1, 1), writes=[T("kT")])

            head_load(0)
            for t in range(4):
                head_tile(t)
            for kc in range(8):
                transpose_kc(kc, g1, T("g1"), *hbufs[0][:2], extra=hbufs[0][2])
            tiles = [(g, t) for g in range(4) for t in range(4)]
            for i, (g, t) in enumerate(tiles):
                if t == 0:
                    tables(g)
                    if g + 1 < 4:
                        head_load(g + 1)
                inproj(g, t)
                if i >= 2:
                    post_b(*tiles[i - 2])
                if b == 0 and late_casts:
                    cast_piece(*late_casts.pop(0))
                if g + 1 < 4:
                    if t < 2:
                        head_tile(2 * t)
                        head_tile(2 * t + 1)
                    else:
                        hd, ht, hx = hbufs[(g + 1) % 2]
                        for kc in range(4 * (t - 2), 4 * (t - 2) + 4):
                            transpose_kc(kc, g1, T("g1"), hd, ht, extra=hx)
                post(g, t)
                if i >= 1:
                    post_add(*tiles[i - 1])
            post_add(*tiles[15])
            post_b(*tiles[14])
            post_b(*tiles[15])

        ctr = {"pt": 0, "ds": 0, "ws": 0, "z1": 0}

        def chunk(b, c, stage=lambda n: None, prev_tail=(), pre_done=0):
            prev_tail = list(prev_tail)
            j0 = c * 512
            r0 = b * SEQ + c * 512
            steps = [(j, kb) for j in range(4) for kb in range(16)]

            def qk_op(s):
                j, kb = steps[s]
                sg = s % 2

                def f(e):
                    e.matmul(ps[:, (2 * sg) * 512:(2 * sg + 1) * 512], lhsT=kT[0:64, kb * 128:(kb + 1) * 128],
                             rhs=qT[0:64, j, j0:j0 + 512], start=True, stop=True)
                    return e.matmul(ps[:, (2 * sg + 1) * 512:(2 * sg + 2) * 512], lhsT=kT[64:128, kb * 128:(kb + 1) * 128],
                                    rhs=qT[64:128, j, j0:j0 + 512], start=True, stop=True)
                P.op(PE, f, reads=[T("kT"), T("qT")], writes=hb(2 * sg, 2))

            def pv_op(s):
                j, kb = steps[s]
                sg = s % 2
                slot = ctr["pt"] % NPT
                ctr["pt"] += 1
                P.op(ACT, lambda e: e.activation(out=PT[slot], in_=ps[:, 2 * sg * 512:(2 * sg + 2) * 512], func=AF.Exp,
                                                 bias=nbias[:], scale=1.0),
                     reads=hb(2 * sg, 2) + [T("nbias"), G2], writes=[T("PT%d" % slot)])

                def f(e):
                    e.matmul(ps[:, 4 * 512:5 * 512], lhsT=vp[:, kb, 0:128], rhs=PT[slot][:, 0:512],
                             start=(kb == 0), stop=(kb == 15))
                    return e.matmul(ps[:, 5 * 512:6 * 512], lhsT=vp[:, kb, 128:256], rhs=PT[slot][:, 512:1024],
                                    start=(kb == 0), stop=(kb == 15))
                P.op(PE, f, reads=[T("vp"), T("PT%d" % slot), G2], writes=hb(4, 2))
                if kb == 15:
                    P.op(DVE, lambda e: e.tensor_copy(ocp[:], ps[:, 4 * 512:6 * 512]), reads=hb(4, 2), writes=[T("ocp")])
                    for hp in range(2):
                        orow = slice(hp * 64, hp * 64 + 64)
                        srow = slice((1 - hp) * 64, (1 - hp) * 64 + 64)
                        cs = slice(hp * 512, (hp + 1) * 512)
                        if j == 3:
                            P.op(ACT, lambda e, orow=orow, srow=srow, cs=cs: e.activation(out=rcp[orow, :], in_=ocp[srow, cs], func=AF.Ln),
                                 reads=[T("ocp")], writes=[T("rcp")])
                            P.op(ACT, lambda e, orow=orow: e.activation(out=rcp[orow, :], in_=rcp[orow, :], func=AF.Exp, scale=-1.0),
                                 reads=[T("rcp")], writes=[T("rcp")])
                        else:
                            P.op(DVE, lambda e, orow=orow, srow=srow, cs=cs: e.reciprocal(rcp[orow, :], ocp[srow, cs]),
                                 reads=[T("ocp")], writes=[T("rcp")])
                        P.op(DVE, lambda e, orow=orow, cs=cs: e.tensor_tensor(
                            out=mixedT[orow, j, :], in0=ocp[orow, cs], in1=rcp[orow, :], op=ALU.mult),
                            reads=[T("ocp"), T("rcp")], writes=[T("mixedT")])

            funits = [(mi, hf, sl) for mi in range(2) for hf in range(2) for sl in range(4)]

            def fourier_micro(m, cn=c):
                u, a = divmod(m, 4)
                mi, hf, sl = funits[u]
                j0 = cn * 512
                if a == 0:
                    dmat = (dftc_d, dfts_d)[mi]
                    slot = ctr["ds"] % NDS
                    ctr["ds"] += 1
                    ctr["cur_ds"] = slot
                    P.dma(SP, lambda e: e.dma_start(
                        out=dsl[slot], in_=dmat[sl * 512:(sl + 1) * 512, j0:j0 + 512].rearrange("(a p) j -> p a j", p=128)),
                        "ds%d" % slot, reads=[G2], writes=[T("dsl%d" % slot)])
                slot = ctr["cur_ds"]
                kb = sl * 4 + a

                def f(e):
                    for ci in range(2):
                        cc = 2 * hf + ci
                        bk = 6 + ci
                        r = e.matmul(ps[:, bk * 512:(bk + 1) * 512], lhsT=U[:, kb, cc * 128:(cc + 1) * 128],
                                     rhs=dsl[slot][:, a, :], start=(kb == 0), stop=(kb == 15))
                    return r
                P.op(PE, f, reads=[T("U"), T("dsl%d" % slot), G2], writes=hb(6, 2))
                if sl == 3 and a == 3:
                    for ci in range(2):
                        cc = 2 * hf + ci
                        bk = 6 + ci
                        ab = mi * 4 + cc
                        P.op(DVE, lambda e, bk=bk, ab=ab: e.tensor_copy(atbt[:, ab, :], ps[:, bk * 512:(bk + 1) * 512]),
                             reads=hb(bk, 1), writes=[T("atbt%d" % ab)])

            qk_op(0)
            qk_op(1)
            nfu = pre_done
            nmic = 4 * len(funits)
            pace = 1 if pre_done == 0 else 2
            for s in range(len(steps)):
                pv_op(s)
                if s + 2 < len(steps):
                    qk_op(s + 2)
                if s >= (11 if pace == 1 else 0) and (s % pace == 0) and nfu < nmic:
                    fourier_micro(nfu)
                    nfu += 1
                if s >= 2 and prev_tail:
                    prev_tail.pop(0)()
            while nfu < nmic:
                fourier_micro(nfu)
                nfu += 1
            while prev_tail:
                prev_tail.pop(0)()
            stage("attn")
            for cc in range(4):
                def f(e, cc=cc):
                    e.matmul(ps[:, cc * 512:(cc + 1) * 512], lhsT=bdm[:, cc, :], rhs=atbt[:, cc, :], start=True, stop=False)
                    return e.matmul(ps[:, cc * 512:(cc + 1) * 512], lhsT=bdm[:, 4 + cc, :], rhs=atbt[:, 4 + cc, :], start=False, stop=True)
                P.op(PE, f, reads=[T("bdm"), T("atbt%d" % cc), T("atbt%d" % (4 + cc))], writes=hb(cc, 1))
                if cc % 2 == 0:
                    P.op(DVE, lambda e, cc=cc: e.tensor_copy(mixedT[:, 4 + cc, :], ps[:, cc * 512:(cc + 1) * 512]),
                         reads=hb(cc, 1), writes=[T("mixedT")])
                else:
                    P.op(ACT, lambda e, cc=cc: e.activation(out=mixedT[:, 4 + cc, :], in_=ps[:, cc * 512:(cc + 1) * 512], func=AF.Copy),
                         reads=hb(cc, 1), writes=[T("mixedT")])

            stage("fourier")
            prefetch = (c + 1 < 4) and stop is None
            npre = [0]
            P.dma(SP, lambda e: e.dma_start(out=xbuf[:], in_=x_d[r0:r0 + 512, :].rearrange("(t p) d -> p t d", p=128)),
                  "xl", writes=[T("xb%d" % t) for t in range(4)])
            for t in range(4):
                pb = (4, 0, 2, 4)[t]

                def f(e, t=t, pb=pb):
                    for m in range(8):
                        for h in range(2):
                            r = e.matmul(ps[:, (pb + h) * 512:(pb + h + 1) * 512], lhsT=mixedT[:, m, t * 128:(t + 1) * 128],
                                         rhs=wout[:, m, h * 512:(h + 1) * 512], start=(m == 0), stop=(m == 7))
                    return r
                P.op(PE, f, reads=[T("mixedT"), T("wout")], writes=hb(pb, 2))
                P.op(DVE, lambda e, t=t, pb=pb: e.tensor_tensor(out=xbuf[:, t, :], in0=ps[:, pb * 512:(pb + 2) * 512],
                                                                 in1=xbuf[:, t, :], op=ALU.add),
                     reads=hb(pb, 2) + [T("xb%d" % t)], writes=[T("xb%d" % t)])
                rms_stats(xbuf[:, t, :], t, [T("xb%d" % t)], "c")
                P.op(DVE, lambda e, t=t: e.tensor_scalar(xn[:, t, :], xbuf[:, t, :], rstd[:, t:t + 1], None, op0=ALU.mult),
                     reads=[T("xb%d" % t), T("rstd%d" % t)], writes=[T("xn%d" % t)])
                if prefetch:
                    for _ in range(4):
                        fourier_micro(npre[0], c + 1)
                        npre[0] += 1
            for kc in range(8):
                transpose_kc(kc, g2, T("g2"), hT, "hT")
                if prefetch:
                    for _ in range(2):
                        fourier_micro(npre[0], c + 1)
                        npre[0] += 1

            stage("outproj")
            for fs in range(8):
                slot = ctr["ws"] % NWS
                ctr["ws"] += 1
                wv = wsl[slot][:, :].rearrange("p (k f) -> p k f", f=512)
                P.dma(SP, lambda e, fs=fs, wv=wv: e.dma_start(
                    out=wv, in_=wup_bf_d[:, fs * 512:(fs + 1) * 512].rearrange("(k p) f -> p k f", p=128)),
                    "ws%d" % slot, reads=[T("wup_bf%d" % i) for i in range(8)], writes=[T("wsl%d" % slot)])
                for fi in range(4):
                    fc = fs * 4 + fi
                    bk = fc % 8

                    def f(e, wv=wv, fi=fi, bk=bk):
                        for kc in range(8):
                            r = e.matmul(ps[:, bk * 512:(bk + 1) * 512], lhsT=wv[:, kc, fi * 128:(fi + 1) * 128],
                                         rhs=hT[:, kc, :], start=(kc == 0), stop=(kc == 7))
                        return r
                    P.op(PE, f, reads=[T("wsl%d" % slot)] + [T("hT%d" % kc) for kc in range(8)], writes=hb(bk, 1))
                    zs = ctr["z1"] % 2
                    ctr["z1"] += 1
                    P.op(ACT, lambda e, bk=bk, zs=zs: e.activation(out=z1[zs][:], in_=ps[:, bk * 512:(bk + 1) * 512], func=AF.Relu),
                         reads=hb(bk, 1), writes=[T("z1_%d" % zs)])
                    P.op(POOL, lambda e, fc=fc, zs=zs: e.tensor_tensor(out=zT[:, fc, :], in0=z1[zs][:], in1=z1[zs][:], op=ALU.mult),
                         reads=[T("z1_%d" % zs), G1], writes=[T("zT")])

            stage("mlpup")
            for dsb in range(8):
                slot = ctr["ws"] % NWS
                ctr["ws"] += 1
                wv = wsl[slot][:, :].rearrange("p (a n) -> p a n", n=1024)
                P.dma(SP, lambda e, dsb=dsb, wv=wv: e.dma_start(
                    out=wv, in_=wdown_bf_d[dsb * 512:(dsb + 1) * 512, :].rearrange("(a p) n -> p a n", p=128)),
                    "ws%d" % slot, reads=[T("wdown_bf%d" % dsb)], writes=[T("wsl%d" % slot)])
                for t in range(4):
                    for h in range(2):
                        bk = t * 2 + h

                        def f(e, wv=wv, dsb=dsb, t=t, h=h, bk=bk):
                            for a in range(4):
                                fc = dsb * 4 + a
                                r = e.matmul(ps[:, bk * 512:(bk + 1) * 512], lhsT=zT[:, fc, t * 128:(t + 1) * 128],
                                             rhs=wv[:, a, h * 512:(h + 1) * 512], start=(fc == 0), stop=(fc == 31))
                            return r
                        P.op(PE, f, reads=[T("wsl%d" % slot), T("zT"), G1], writes=hb(bk, 1))
            for t in range(4):
                P.op(DVE, lambda e, t=t: e.tensor_tensor(out=xbuf[:, t, :], in0=ps[:, 2 * t * 512:(2 * t + 2) * 512],
                                                          in1=xbuf[:, t, :], op=ALU.add),
                     reads=hb(2 * t, 2) + [T("xb%d" % t)], writes=[T("xb%d" % t)])
            tail = []
            for t in range(4):
                def stats(t=t):
                    rms_stats(xbuf[:, t, :], t, [T("xb%d" % t)], "f")

                def scale(t=t):
                    P.op(DVE, lambda e: e.scalar_tensor_tensor(out=xbuf[:, t, :], in0=xbuf[:, t, :], scalar=rstd[:, t:t + 1],
                                                               in1=gf_bc[:], op0=ALU.mult, op1=ALU.mult),
                         reads=[T("xb%d" % t), T("rstd%d" % t), T("gf")], writes=[T("xb%d" % t)])
                tail.append(stats)
                tail.append(scale)

            def store():
                P.dma(SP, lambda e: e.dma_start(out=out_d[r0:r0 + 512, :].rearrange("(t p) d -> p t d", p=128), in_=xbuf[:]),
                      "os", reads=[T("xb%d" % t) for t in range(4)], final=True)
            tail.append(store)
            if stop == "chunk":
                for f_ in tail:
                    f_()
                tail = []
            stage("chunk")
            return tail, npre[0]

        def stage(name):
            if stop == name:
                raise _Stop()

        try:
            stage("setup")
            for b in range(nseq):
                fence()
                phase_a(b)
                while b == 0 and late_casts:
                    cast_piece(*late_casts.pop(0))
                stage("phaseA")
                fence()
                tail, npre_ = [], 0
                for c in range(4):
                    tail, npre_ = chunk(b, c, stage, tail, npre_)
                for f_ in tail:
                    f_()
        except _Stop:
            pass
        if stop is not None:
            for e_ in ENGINES:
                for o_ in P.ops[e_]:
                    if o_.is_dma and o_ not in P.final_waits:
                        P.final_waits.append(o_)
        if dumps:
            allb = list(B.values()) + HB
            loc = dict(locals())
            for nm in dumps:
                ap = loc[nm]
                shp = list(ap.shape)
                dd = nc.dram_tensor("dbg_" + nm, shp, ap.dtype, kind="ExternalOutput").ap()
                src = ap if isinstance(ap, bass.AP) else ap[:]
                P.dma(SP, lambda e, dd=dd, src=src: e.dma_start(out=dd, in_=src), "dbg_" + nm, reads=allb, final=True)
        P.emit(nc, st)
    return nc


def _constants():
    bf = ml_dtypes.bfloat16
    k = np.arange(SEQ, dtype=np.int64)
    kj = (k[:, None] * k[None, :]) % SEQ
    ang = kj.astype(np.float64) * (2.0 * np.pi / SEQ)
    dftc = np.cos(ang).astype(np.float32).astype(bf)
    dfts = np.sin(ang).astype(np.float32).astype(bf)
    c = np.arange(64, dtype=np.int64)
    cang = ((c[:, None] * c[None, :]) % 64).astype(np.float64) * (2.0 * np.pi / 64)
    norm = 1.0 / np.sqrt(SEQ * 64.0)
    cc = np.cos(cang) * norm
    sc = -np.sin(cang) * norm
    z = np.zeros((64, 64))
    ccbd = np.block([[cc, z], [z, cc]]).astype(np.float32).astype(bf)
    scbd = np.block([[sc, z], [z, sc]]).astype(np.float32).astype(bf)
    s = np.arange(SEQ)
    row = (s // 64).astype(np.float32)
    col = (s % 64).astype(np.float32)
    inv = (np.float32(10000.0) ** (-np.arange(0, 32, 2, dtype=np.float32) / np.float32(32))).astype(np.float32)
    ra = row[:, None] * inv[None, :]
    ca = col[:, None] * inv[None, :]
    ropec = np.concatenate([np.cos(ra), np.cos(ra), np.cos(ca), np.cos(ca)], axis=1).astype(np.float32)
    ropes = np.concatenate([-np.sin(ra), np.sin(ra), -np.sin(ca), np.sin(ca)], axis=1).astype(np.float32)
    return dftc, dfts, ccbd, scbd, ropec, ropes


def _swap(g):
    return np.concatenate([g[16:32], g[0:16], g[48:64], g[32:48]])


_CACHE = {}


def kernel(x, mix_norm_g, w_in, q_norm_g, k_norm_g, w_fourier, w_out,
           mlp_norm_g, w_up, w_down, final_norm_g):
    f32 = np.float32
    x = np.asarray(x, f32)
    w_in = np.asarray(w_in, f32)
    w_out = np.asarray(w_out, f32)
    if "nc" not in _CACHE:
        _CACHE["nc"] = build_nc()
        _CACHE["const"] = _constants()
    nc = _CACHE["nc"]
    dftc, dfts, ccbd, scbd, ropec, ropes = _CACHE["const"]
    qcols = np.concatenate([np.concatenate([np.arange(j * 64, (j + 1) * 64), np.arange((j + 4) * 64, (j + 5) * 64)])
                            for j in range(4)])
    cols = np.concatenate([qcols, np.arange(512, 1280)])
    w_in_p = np.ascontiguousarray(w_in[:, cols])
    rows = np.concatenate([qcols, np.arange(512, 1024)])
    w_out_p = np.ascontiguousarray(w_out[rows, :])
    gq = np.asarray(q_norm_g, f32)
    gk = np.asarray(k_norm_g, f32)
    g_qk = np.concatenate([gq, _swap(gq), gk, _swap(gk)]).reshape(1, 256).astype(f32)
    common = {
        "w_in": w_in_p, "w_out": w_out_p,
        "w_up": np.ascontiguousarray(np.asarray(w_up, f32)),
        "w_down": np.ascontiguousarray(np.asarray(w_down, f32)),
        "wf": np.ascontiguousarray(np.asarray(w_fourier, f32).reshape(512, 64)),
        "g_mix": np.asarray(mix_norm_g, f32).reshape(1, DM),
        "g_mlp": np.asarray(mlp_norm_g, f32).reshape(1, DM),
        "g_fin": np.asarray(final_norm_g, f32).reshape(1, DM),
        "g_qk": g_qk,
        "dftc": dftc, "dfts": dfts, "ropec": ropec, "ropes": ropes, "ccbd": ccbd, "scbd": scbd,
    }
    in_maps = []
    for c in range(N_CORES):
        m = dict(common)
        m["x"] = np.ascontiguousarray(x[c * NSEQ:(c + 1) * NSEQ].reshape(NSEQ * SEQ, DM))
        in_maps.append(m)
    res = run_bass_kernel_spmd(nc, in_maps, core_ids=list(range(N_CORES)))
    outs = [np.asarray(r["out"], f32).reshape(NSEQ, SEQ, DM) for r in res.results]
    return np.concatenate(outs, axis=0)
```
